# Optimizing a Trainium2 kernel written in Bass

```python
import jax, jax.numpy as jnp
from jax import lax
import numpy as np

D_MODEL = 1024
BATCH = 8
SEQ = 2048
DEPTH = 2
DEC_BATCH = 128
DEC_SEQ = 4
PAST_LEN = 16384
PAGE_SIZE = 128

N_MIXERS = 2
N_CONV_LAYERS = (DEPTH + 1) // 2
N_SSM_LAYERS = DEPTH // 2
CONV_WIDTH = 31
CONV_BUF = CONV_WIDTH - 1
SSM_GROUP = 16
SSM_GROUPS = D_MODEL // SSM_GROUP
SSM_STATE = 64
SCAN_BLOCK = 128
D_FF = 4 * D_MODEL
N_MEM = 256
XATTN_HEADS = 4
XATTN_HEAD_DIM = D_MODEL // XATTN_HEADS
EPS = 1e-6
DT_MIN = 1e-3
DT_MAX = 1e-1

kernel_name = 'hybrid_conformer_conv_s5_memxattn_step'


def _rmsnorm(x, g):
    x32 = x.astype(jnp.float32)
    y = x32 * lax.rsqrt(jnp.mean(x32 * x32, axis=-1, keepdims=True) + EPS)
    return (y * g.astype(jnp.float32)).astype(x.dtype)


def _conformer_conv(u, buf, w_in, b_in, dw, dw_b, ln_g, ln_b, w_out):
    h = jnp.einsum('bld,de->ble', u, w_in) + b_in
    g = h[..., :D_MODEL] * jax.nn.sigmoid(h[..., D_MODEL:])
    padded = jnp.concatenate([buf.astype(g.dtype), g], axis=1)
    c = lax.conv_general_dilated(
        padded, dw[:, None, :].astype(g.dtype), window_strides=(1,), padding='VALID',
        dimension_numbers=('NWC', 'WIO', 'NWC'), feature_group_count=D_MODEL) + dw_b
    c32 = c.astype(jnp.float32)
    mu = jnp.mean(c32, axis=-1, keepdims=True)
    var = jnp.mean(jnp.square(c32 - mu), axis=-1, keepdims=True)
    n = (c32 - mu) * lax.rsqrt(var + EPS) * ln_g.astype(jnp.float32) + ln_b.astype(jnp.float32)
    s = jax.nn.silu(n).astype(u.dtype)
    out = jnp.einsum('bld,de->ble', s, w_out)
    return out, padded[:, -CONV_BUF:]


def _cplx_affine_combine(e1, e2):
    a1r, a1i, b1r, b1i = e1
    a2r, a2i, b2r, b2i = e2
    return (a2r * a1r - a2i * a1i,
            a2r * a1i + a2i * a1r,
            a2r * b1r - a2i * b1i + b2r,
            a2r * b1i + a2i * b1r + b2i)


def _s5(u, h0_re, h0_im, a_re, a_im, log_dt, b_re, b_im, c_re, c_im, d_skip, w_glu):
    f32 = jnp.float32
    bsz, seq_len, _ = u.shape
    blk = SCAN_BLOCK if seq_len % SCAN_BLOCK == 0 else seq_len
    n_blk = seq_len // blk
    a_re = a_re.astype(f32)
    a_im = a_im.astype(f32)
    b_re = b_re.astype(f32)
    b_im = b_im.astype(f32)
    c_re = c_re.astype(f32)
    c_im = c_im.astype(f32)
    dt = jnp.exp(log_dt.astype(f32))[:, None]
    mag = jnp.exp(dt * a_re)
    ab_re = mag * jnp.cos(dt * a_im)
    ab_im = mag * jnp.sin(dt * a_im)
    den = a_re * a_re + a_im * a_im
    num_re = ab_re - 1.0
    coef_re = (num_re * a_re + ab_im * a_im) / den
    coef_im = (ab_im * a_re - num_re * a_im) / den
    bb_re = coef_re[..., None] * b_re - coef_im[..., None] * b_im
    bb_im = coef_re[..., None] * b_im + coef_im[..., None] * b_re
    d = d_skip.astype(f32).reshape(SSM_GROUPS, SSM_GROUP)
    ug = u.astype(f32).reshape(bsz, n_blk, blk, SSM_GROUPS, SSM_GROUP).transpose(1, 0, 2, 3, 4)

    def block_step(carry, u_blk):
        h_re, h_im = carry
        bu_re = jnp.einsum('bcgi,gpi->bcgp', u_blk, bb_re)
        bu_im = jnp.einsum('bcgi,gpi->bcgp', u_blk, bb_im)
        bu_re = bu_re.at[:, 0].add(ab_re * h_re - ab_im * h_im)
        bu_im = bu_im.at[:, 0].add(ab_re * h_im + ab_im * h_re)
        a_r = jnp.broadcast_to(ab_re, bu_re.shape)
        a_i = jnp.broadcast_to(ab_im, bu_im.shape)
        _, _, s_re, s_im = lax.associative_scan(_cplx_affine_combine, (a_r, a_i, bu_re, bu_im), axis=1)
        y = (jnp.einsum('bcgp,gip->bcgi', s_re, c_re)
             - jnp.einsum('bcgp,gip->bcgi', s_im, c_im) + d * u_blk)
        return (s_re[:, -1], s_im[:, -1]), y

    (h_re, h_im), ys = lax.scan(block_step, (h0_re.astype(f32), h0_im.astype(f32)), ug)
    y = ys.transpose(1, 0, 2, 3, 4).reshape(bsz, seq_len, D_MODEL)
    z = jnp.einsum('bld,de->ble', jax.nn.gelu(y).astype(u.dtype), w_glu)
    out = z[..., :D_MODEL] * jax.nn.sigmoid(z[..., D_MODEL:])
    return out, h_re, h_im


def _mem_kv(mem, w_k, w_v):
    bsz = mem.shape[0]
    k = jnp.einsum('bmd,de->bme', mem, w_k).reshape(bsz, N_MEM, XATTN_HEADS, XATTN_HEAD_DIM)
    v = jnp.einsum('bmd,de->bme', mem, w_v).reshape(bsz, N_MEM, XATTN_HEADS, XATTN_HEAD_DIM)
    return k, v


def _cross_attention(xn, mem_k, mem_v, w_q, w_o):
    bsz, seq_len, _ = xn.shape
    q = jnp.einsum('bld,de->ble', xn, w_q).reshape(bsz, seq_len, XATTN_HEADS, XATTN_HEAD_DIM)
    s = jnp.einsum('blhe,bmhe->bhlm', q.astype(jnp.float32), mem_k.astype(jnp.float32)) * (XATTN_HEAD_DIM ** -0.5)
    p = jax.nn.softmax(s, axis=-1)
    o = jnp.einsum('bhlm,bmhe->blhe', p, mem_v.astype(jnp.float32)).reshape(bsz, seq_len, D_MODEL)
    return jnp.einsum('bld,de->ble', o.astype(xn.dtype), w_o)


def _sqrelu_mlp(xn, w_up, w_down):
    h = jax.nn.relu(jnp.einsum('bld,df->blf', xn, w_up))
    return jnp.einsum('blf,fd->bld', h * h, w_down)


def _trunk(x, conv_bufs, s5_re, s5_im, mem_k, mem_v, p):
    new_conv, new_re, new_im = [], [], []
    for i in range(DEPTH):
        j = i // N_MIXERS
        h = _rmsnorm(x, p['norm_mix'][i])
        if i % N_MIXERS == 0:
            out, nb = _conformer_conv(h, conv_bufs[j], p['conv_w_in'][j], p['conv_b_in'][j], p['conv_dw'][j],
                                      p['conv_dw_b'][j], p['conv_ln_g'][j], p['conv_ln_b'][j], p['conv_w_out'][j])
            new_conv.append(nb)
        else:
            out, hr, hi = _s5(h, s5_re[j], s5_im[j], p['s5_a_re'][j], p['s5_a_im'][j], p['s5_log_dt'][j],
                              p['s5_b_re'][j], p['s5_b_im'][j], p['s5_c_re'][j], p['s5_c_im'][j],
                              p['s5_d'][j], p['s5_w_glu'][j])
            new_re.append(hr)
            new_im.append(hi)
        x = x + out
        x = x + _cross_attention(_rmsnorm(x, p['norm_xattn'][i]), mem_k[i], mem_v[i],
                                 p['xattn_w_q'][i], p['xattn_w_o'][i])
        x = x + _sqrelu_mlp(_rmsnorm(x, p['norm_ffn'][i]), p['mlp_w_up'][i], p['mlp_w_down'][i])
    y = _rmsnorm(x, p['norm_final'])
    return y, jnp.stack(new_conv), jnp.stack(new_re), jnp.stack(new_im)


def setup_inputs(seed: int = 0) -> dict:
    key = jax.random.key(seed)
    ks = list(jax.random.split(key, 40))
    f32 = jnp.float32

    def nrm(shape, scale):
        return jax.random.normal(ks.pop(), shape, f32) * scale

    inp = {}
    inp['x_prompt'] = nrm((BATCH, SEQ, D_MODEL), 1.0)
    inp['x_sample'] = nrm((DEC_BATCH, DEC_SEQ, D_MODEL), 1.0)
    inp['mem_prompt'] = nrm((BATCH, N_MEM, D_MODEL), 1.0)
    inp['cache_conv'] = nrm((N_CONV_LAYERS, DEC_BATCH, CONV_BUF, D_MODEL), 0.5)
    inp['state_s5_re'] = nrm((N_SSM_LAYERS, DEC_BATCH, SSM_GROUPS, SSM_STATE), 0.1)
    inp['state_s5_im'] = nrm((N_SSM_LAYERS, DEC_BATCH, SSM_GROUPS, SSM_STATE), 0.1)
    inp['cache_mem_k'] = nrm((DEPTH, DEC_BATCH, N_MEM, XATTN_HEADS, XATTN_HEAD_DIM), 1.0)
    inp['cache_mem_v'] = nrm((DEPTH, DEC_BATCH, N_MEM, XATTN_HEADS, XATTN_HEAD_DIM), 1.0)
    inp['norm_mix'] = 1.0 + nrm((DEPTH, D_MODEL), 0.01)
    inp['norm_xattn'] = 1.0 + nrm((DEPTH, D_MODEL), 0.01)
    inp['norm_ffn'] = 1.0 + nrm((DEPTH, D_MODEL), 0.01)
    inp['norm_final'] = 1.0 + nrm((D_MODEL,), 0.01)
    inp['conv_w_in'] = nrm((N_CONV_LAYERS, D_MODEL, 2 * D_MODEL), D_MODEL ** -0.5)
    inp['conv_b_in'] = nrm((N_CONV_LAYERS, 2 * D_MODEL), 0.01)
    inp['conv_dw'] = nrm((N_CONV_LAYERS, CONV_WIDTH, D_MODEL), CONV_WIDTH ** -0.5)
    inp['conv_dw_b'] = nrm((N_CONV_LAYERS, D_MODEL), 0.01)
    inp['conv_ln_g'] = 1.0 + nrm((N_CONV_LAYERS, D_MODEL), 0.01)
    inp['conv_ln_b'] = nrm((N_CONV_LAYERS, D_MODEL), 0.01)
    inp['conv_w_out'] = nrm((N_CONV_LAYERS, D_MODEL, D_MODEL), D_MODEL ** -0.5)
    inp['s5_a_re'] = -0.5 + nrm((N_SSM_LAYERS, SSM_GROUPS, SSM_STATE), 0.01)
    inp['s5_a_im'] = jnp.pi * jnp.arange(SSM_STATE, dtype=f32) + nrm((N_SSM_LAYERS, SSM_GROUPS, SSM_STATE), 0.01)
    inp['s5_log_dt'] = jax.random.uniform(ks.pop(), (N_SSM_LAYERS, SSM_GROUPS), f32,
                                          float(np.log(DT_MIN)), float(np.log(DT_MAX)))
    inp['s5_b_re'] = nrm((N_SSM_LAYERS, SSM_GROUPS, SSM_STATE, SSM_GROUP), (2 * SSM_GROUP) ** -0.5)
    inp['s5_b_im'] = nrm((N_SSM_LAYERS, SSM_GROUPS, SSM_STATE, SSM_GROUP), (2 * SSM_GROUP) ** -0.5)
    inp['s5_c_re'] = nrm((N_SSM_LAYERS, SSM_GROUPS, SSM_GROUP, SSM_STATE), SSM_STATE ** -0.5)
    inp['s5_c_im'] = nrm((N_SSM_LAYERS, SSM_GROUPS, SSM_GROUP, SSM_STATE), SSM_STATE ** -0.5)
    inp['s5_d'] = nrm((N_SSM_LAYERS, D_MODEL), 1.0)
    inp['s5_w_glu'] = nrm((N_SSM_LAYERS, D_MODEL, 2 * D_MODEL), D_MODEL ** -0.5)
    inp['xattn_w_q'] = nrm((DEPTH, D_MODEL, D_MODEL), D_MODEL ** -0.5)
    inp['xattn_w_k'] = nrm((DEPTH, D_MODEL, D_MODEL), D_MODEL ** -0.5)
    inp['xattn_w_v'] = nrm((DEPTH, D_MODEL, D_MODEL), D_MODEL ** -0.5)
    inp['xattn_w_o'] = nrm((DEPTH, D_MODEL, D_MODEL), D_MODEL ** -0.5)
    inp['mlp_w_up'] = nrm((DEPTH, D_MODEL, D_FF), D_MODEL ** -0.5)
    inp['mlp_w_down'] = nrm((DEPTH, D_FF, D_MODEL), D_FF ** -0.5)
    return inp


def reference(x_prompt, x_sample, mem_prompt, cache_conv, state_s5_re, state_s5_im, cache_mem_k, cache_mem_v,
              norm_mix, norm_xattn, norm_ffn, norm_final,
              conv_w_in, conv_b_in, conv_dw, conv_dw_b, conv_ln_g, conv_ln_b, conv_w_out,
              s5_a_re, s5_a_im, s5_log_dt, s5_b_re, s5_b_im, s5_c_re, s5_c_im, s5_d, s5_w_glu,
              xattn_w_q, xattn_w_k, xattn_w_v, xattn_w_o, mlp_w_up, mlp_w_down):
    p = {'norm_mix': norm_mix, 'norm_xattn': norm_xattn, 'norm_ffn': norm_ffn, 'norm_final': norm_final,
         'conv_w_in': conv_w_in, 'conv_b_in': conv_b_in, 'conv_dw': conv_dw, 'conv_dw_b': conv_dw_b,
         'conv_ln_g': conv_ln_g, 'conv_ln_b': conv_ln_b, 'conv_w_out': conv_w_out,
         's5_a_re': s5_a_re, 's5_a_im': s5_a_im, 's5_log_dt': s5_log_dt, 's5_b_re': s5_b_re,
         's5_b_im': s5_b_im, 's5_c_re': s5_c_re, 's5_c_im': s5_c_im, 's5_d': s5_d, 's5_w_glu': s5_w_glu,
         'xattn_w_q': xattn_w_q, 'xattn_w_o': xattn_w_o, 'mlp_w_up': mlp_w_up, 'mlp_w_down': mlp_w_down}
    bp = x_prompt.shape[0]
    kv = [_mem_kv(mem_prompt, xattn_w_k[i], xattn_w_v[i]) for i in range(DEPTH)]
    mem_k_prompt = jnp.stack([kv[i][0] for i in range(DEPTH)])
    mem_v_prompt = jnp.stack([kv[i][1] for i in range(DEPTH)])
    conv0 = jnp.zeros((N_CONV_LAYERS, bp, CONV_BUF, D_MODEL), x_prompt.dtype)
    s50 = jnp.zeros((N_SSM_LAYERS, bp, SSM_GROUPS, SSM_STATE), jnp.float32)
    y_prompt, conv_prompt, s5_re_prompt, s5_im_prompt = _trunk(
        x_prompt, conv0, s50, s50, mem_k_prompt, mem_v_prompt, p)
    y_sample, conv_sample, s5_re_sample, s5_im_sample = _trunk(
        x_sample, cache_conv, state_s5_re, state_s5_im, cache_mem_k, cache_mem_v, p)
    return (y_prompt, y_sample, conv_prompt, conv_sample, s5_re_prompt, s5_im_prompt,
            s5_re_sample, s5_im_sample, mem_k_prompt, mem_v_prompt)
```

```python
import contextlib
import math
import numpy as np
import concourse.bass as bass
import concourse.mybir as mybir
from concourse.bass_utils import run_bass_kernel_spmd

F32 = mybir.dt.float32
BF16 = mybir.dt.bfloat16
AF = mybir.ActivationFunctionType
ALU = mybir.AluOpType

N_DMA_SEMS = 24


class Trk:
    __slots__ = ("last_w", "readers")

    def __init__(self):
        self.last_w = None
        self.readers = []


def _flat(x):
    out = []
    for t in x:
        if isinstance(t, (list, tuple)):
            out.extend(_flat(t))
        else:
            out.append(t)
    return out


class Q:
    def __init__(self, name):
        self.name = name
        self.ins = []
        self.waited = {}
        self.sem = None


class Sched:
    def __init__(self, nc):
        self.nc = nc
        self.q = {n: Q(n) for n in ("pe", "act", "dve", "pool", "sp")}
        self.dma_cnt = [0] * N_DMA_SEMS
        self.dma_rr2 = {0: 0, N_DMA_SEMS // 2: 0}
        self.open_dmas = []

    def _deps(self, reads, writes):
        deps = []
        for t in reads:
            if t.last_w is not None:
                deps.append((t.last_w, "raw"))
        for t in writes:
            if t.last_w is not None:
                deps.append((t.last_w, "waw"))
            for r in t.readers:
                deps.append((r, "war"))
        return deps

    def _add_waits(self, q, deps):
        waits = []
        for tok, kind in deps:
            if tok[0] == "e":
                _, qn, idx = tok
                if qn == q.name and qn == "pe":
                    continue
                key = ("e", qn)
                if q.waited.get(key, -1) >= idx:
                    continue
                q.waited[key] = idx
                self.q[qn].ins[idx][2] = True
                waits.append(tok)
            else:
                _, k, v = tok
                key = ("d", k)
                if q.waited.get(key, -1) >= v:
                    continue
                q.waited[key] = v
                waits.append(tok)
        return waits

    def _update(self, tok, reads, writes):
        for t in writes:
            t.last_w = tok
            t.readers = []
        for t in reads:
            if t in writes:
                continue
            if tok[0] == "e":
                t.readers = [r for r in t.readers if not (r[0] == "e" and r[1] == tok[1])]
            t.readers.append(tok)

    def emit(self, qn, fn, reads=(), writes=()):
        reads, writes = _flat(reads), _flat(writes)
        q = self.q[qn]
        waits = self._add_waits(q, self._deps(reads, writes))
        idx = len(q.ins)
        q.ins.append([fn, waits, False, None])
        tok = ("e", qn, idx)
        self._update(tok, reads, writes)
        return tok

    def barrier(self):
        lasts = {}
        for qn, q in self.q.items():
            for idx in range(len(q.ins) - 1, -1, -1):
                if q.ins[idx][0] is not None and q.ins[idx][3] is None:
                    lasts[qn] = idx
                    break
        for qn, q in self.q.items():
            deps = [(("e", q2n, idx), "raw") for q2n, idx in lasts.items() if q2n != qn]
            deps += [(t, "raw") for t in self.open_dmas]
            waits = self._add_waits(q, deps)
            q.ins.append([None, waits, False, None])
        self.open_dmas = []

    def dma(self, qn, out, in_, reads=(), writes=(), track=True):
        reads, writes = _flat(reads), _flat(writes)
        q = self.q[qn]
        half = N_DMA_SEMS // 2
        base = half if qn == "pool" else 0
        k = base + self.dma_rr2[base]
        self.dma_rr2[base] = (self.dma_rr2[base] + 1) % half
        self.dma_cnt[k] += 16
        v = self.dma_cnt[k]
        deps = self._deps(reads, writes)
        if v > 16:
            deps.append((("d", k, v - 16), "raw"))
        waits = self._add_waits(q, deps)
        q.ins.append([lambda e: e.dma_start(out=out, in_=in_), waits, False, k])
        tok = ("d", k, v)
        if track:
            self.open_dmas.append(tok)
        self._update(tok, reads, writes)
        return tok

    def wait_all(self, qn, toks):
        q = self.q[qn]
        waits = self._add_waits(q, [(t, "raw") for t in toks])
        q.ins.append([None, waits, False, None])

    def run(self, st):
        nc = self.nc
        for qn, q in self.q.items():
            q.sem = st.enter_context(nc.semaphore("s_" + qn))
        dsems = [st.enter_context(nc.semaphore("d%d" % k)) for k in range(N_DMA_SEMS)]
        for q in self.q.values():
            c = 0
            q.pref = []
            for ins in q.ins:
                if ins[2]:
                    c += 1
                q.pref.append(c)
        st.enter_context(nc.allow_non_contiguous_dma(reason="small one-time parameter loads"))
        block = st.enter_context(nc.Block())

        def body(q):
            def f(e):
                for fn, waits, marked, dk in q.ins:
                    for w in waits:
                        if w[0] == "e":
                            q2 = self.q[w[1]]
                            e.wait_ge(q2.sem, q2.pref[w[2]])
                        else:
                            e.wait_ge(dsems[w[1]], w[2])
                    if fn is None:
                        continue
                    r = fn(e)
                    if dk is not None:
                        r.then_inc(dsems[dk], 16)
                    elif marked:
                        r.then_inc(q.sem, 1)
            return f

        block.sync(body(self.q["sp"]))
        block.tensor(body(self.q["pe"]))
        block.scalar(body(self.q["act"]))
        block.vector(body(self.q["dve"]))
        block.gpsimd(body(self.q["pool"]))


D = 1024
NT = 512
EPS = 1e-6
MAGIC = 12582912.0
TWO_PI = 2.0 * math.pi
C1 = 6.28125
C2 = TWO_PI - C1
SC2PI = TWO_PI * (1.0 - 2e-6)

W_IN = [
    ("norm_mix", [2, D]), ("norm_xattn", [2, D]), ("norm_ffn", [2, D]), ("norm_final", [D]),
    ("conv_w_in", [1, D, 2 * D]), ("conv_b_in", [1, 2 * D]), ("conv_dw", [1, 31, D]), ("conv_dw_b", [1, D]),
    ("conv_ln_g", [1, D]), ("conv_ln_b", [1, D]), ("conv_w_out", [1, D, D]),
    ("s5_a_re", [1, 64, 64]), ("s5_a_im", [1, 64, 64]), ("s5_log_dt", [1, 64]),
    ("s5_b_re", [1, 64, 64, 16]), ("s5_b_im", [1, 64, 64, 16]), ("s5_c_re", [1, 64, 16, 64]),
    ("s5_c_im", [1, 64, 16, 64]), ("s5_d", [1, D]), ("s5_w_glu", [1, D, 2 * D]),
    ("xattn_w_q", [2, D, D]), ("xattn_w_k", [2, D, D]), ("xattn_w_v", [2, D, D]), ("xattn_w_o", [2, D, D]),
    ("mlp_w_up", [2, D, 4 * D]), ("mlp_w_down", [2, 4 * D, D]),
]


DEBUG = False
S5_REORDER = 2


def build_nc(plan=None):
    record = plan is None
    nc = bass.Bass("TRN2", target_bir_lowering=False)
    dbg = nc.dram_tensor("dbg", [8, 128, 8, NT], F32, kind="ExternalOutput").ap() if DEBUG else None
    dbg_n = [0]
    di = lambda n, s: nc.dram_tensor(n, s, F32, kind="ExternalInput").ap()
    do = lambda n, s: nc.dram_tensor(n, s, F32, kind="ExternalOutput").ap()
    xp, xs, mem = di("xp", [2048, D]), di("xs", [64, D]), di("mem", [256, D])
    cc, sre, sim = di("cc", [16, 30, D]), di("sre", [16, 4096]), di("sim", [16, 4096])
    cmk, cmv = di("cmk", [2, 16, 256, D]), di("cmv", [2, 16, 256, D])
    w = {n: di(n, s) for n, s in W_IN}
    c_ident, c_iota, c_iotas = di("c_ident", [128, 128]), di("c_iota", [128, 512]), di("c_iotas", [128, 64])
    c_sgn, c_ja, c_jb = di("c_sgn", [128, 4]), di("c_ja", [128, 128]), di("c_jb", [128, 128])
    yp, ys = do("yp", [2048, D]), do("ys", [64, D])
    convp, convs = do("convp", [30, D]), do("convs", [16, 30, D])
    s5rp, s5ip = do("s5rp", [64, 64]), do("s5ip", [64, 64])
    s5rs, s5is = do("s5rs", [16, 4096]), do("s5is", [16, 4096])
    mkp, mvp = do("mkp", [2, 256, D]), do("mvp", [2, 256, D])

    st = contextlib.ExitStack()
    S = Sched(nc)
    out_toks = []

    def sb(name, shape, dt=F32):
        return st.enter_context(nc.sbuf_tensor(name, shape, dt))

    V = lambda fn, r=(), w_=(): S.emit("dve", fn, r, w_)
    A = lambda fn, r=(), w_=(): S.emit("act", fn, r, w_)
    G = lambda fn, r=(), w_=(): S.emit("pool", fn, r, w_)
    PE = lambda fn, r=(), w_=(): S.emit("pe", fn, r, w_)

    R1B = 25600

    class Ctx:
        pass

    def make_ctx(ci):
        C = Ctx()
        C.xres = sb("xres%d" % ci, [128, 8, NT]); C.t_xres = Trk()
        C.xn = sb("xn%d" % ci, [128, 8, NT], BF16); C.t_xn = Trk()
        C.actb = sb("actb%d" % ci, [128, 8, NT], BF16); C.t_actb = Trk()
        R1 = sb("R1_%d" % ci, [128, R1B // 4])
        slots = [Trk() for _ in range(R1B // 1024)]

        def r1(off, nbytes, dt=F32):
            v = R1[:, off // 4:(off + nbytes) // 4]
            return v if dt == F32 else v.bitcast(dt)

        def rt(off, nbytes):
            return slots[off // 1024:(off + nbytes + 1023) // 1024]
        C.r1, C.rt = r1, rt
        C.hid = r1(0, 16384, BF16).rearrange("p (m n) -> p m n", m=16); C.t_hid = rt(0, 16384)
        C.cf = r1(0, 16384).rearrange("p (m n) -> p m n", m=8); C.t_cf = rt(0, 16384)
        C.gpad = r1(16384, 8 * 542 * 2, BF16).rearrange("p (m n) -> p m n", m=8); C.t_gpad = rt(16384, 8704)
        C.gpads = r1(16384, 8 * 16 * 34 * 2, BF16).rearrange("p (m b n) -> p m b n", m=8, b=16); C.t_gpads = C.t_gpad
        C.sq = r1(16384, 8192, BF16).rearrange("p (m n) -> p m n", m=8); C.t_sq = rt(16384, 8192)
        C.bU = sb("bU%d" % ci, [128, 64]); C.t_bU = Trk()
        C.Kc = [r1(0, 4096, BF16).rearrange("p (m c) -> p m c", m=2)]; C.t_Kc = [rt(0, 4096)]
        C.Vc = [r1(4096, 4096, BF16).rearrange("p (m c) -> p m c", m=2)]; C.t_Vc = [rt(4096, 4096)]
        C.KTs = [r1(8192, 4096, BF16).rearrange("p (f m) -> p f m", f=8)]; C.t_KTs = [rt(8192, 4096)]
        C.s5w = [r1(j * 2048, 2048) for j in range(10)]; C.t_s5w = [rt(j * 2048, 2048) for j in range(10)]
        C.qcb = [r1(20480 + 2048 * i, 1024, BF16) for i in range(2)]; C.qsb = [r1(21504 + 2048 * i, 1024, BF16) for i in range(2)]
        C.t_qcs = [rt(20480 + 2048 * i, 2048) for i in range(2)]
        C.s5ws = [r1(j * 256, 256) for j in range(10)]; C.t_s5ws = [rt(j * 256, 256) for j in range(10)]
        C.H0 = r1(4096, 4096).rearrange("p (g b) -> p g b", g=64); C.t_H0 = rt(4096, 4096)
        C.QCLs = r1(8192, 4096).rearrange("p (g b) -> p g b", g=64); C.QSLs = r1(12288, 4096).rearrange("p (g b) -> p g b", g=64)
        C.t_QLs = rt(8192, 8192)
        C.HBr = r1(16384, 4096).rearrange("p (g c) -> p g c", g=8)[0:16]; C.t_HBr = rt(16384, 4096)
        C.osm = r1(16384, 2048).rearrange("p (g c) -> p g c", g=4)[0:16]; C.t_osm = rt(16384, 2048)
        return C

    ctxs = [make_ctx(0), make_ctx(1)]
    SC = ctxs[1]
    NW = 3
    wring = [sb("w%d" % i, [128, 4096], BF16) for i in range(NW)]; t_w = [Trk() for _ in range(NW)]
    stage = [sb("stg%d" % i, [128, D]) for i in range(2)]; t_stage = [Trk(), Trk()]
    tmpf = [sb("tmpf%d" % i, [128, NT]) for i in range(4)]; t_tmpf = [Trk() for _ in range(4)]
    tmpb = [sb("tmpb%d" % i, [128, NT], BF16) for i in range(4)]; t_tmpb = [Trk() for _ in range(4)]
    ident_f = sb("ident_f", [128, 128]); ident_b = sb("ident_b", [128, 128], BF16); ones_b = sb("ones_b", [128, 128], BF16)
    ja_f = sb("ja_f", [128, 128]); jb_f = sb("jb_f", [128, 128])
    iota = sb("iota", [128, 512]); iotas = sb("iotas", [128, 64]); sgn = sb("sgn", [128, 4])
    t_const = Trk()
    gains = sb("gains", [128, 7, 8])
    cvec = sb("cvec", [128, 5, 16])
    dwt = sb("dwt", [128, 8, 31])
    diag = [sb("diag%d" % i, [128, 128], BF16) for i in range(4)]; t_diag = [Trk() for _ in range(4)]
    hist = sb("hist", [128, 8, 30], BF16); t_hist = Trk()
    gl30 = sb("gl30", [128, 8, 64]); t_gl30 = Trk()
    memT = SC.r1(20480, 4096, BF16).rearrange("p (k m) -> p k m", k=8); t_memT = SC.rt(20480, 4096)
    KT = [sb("KT%d" % l, [128, 8, 256], BF16) for l in range(2)]; t_KT = [Trk(), Trk()]
    Vb = [sb("Vb%d" % l, [128, 2, D], BF16) for l in range(2)]; t_Vb = [Trk(), Trk()]
    Eb = [sb("Eb%d" % i, [128, NT], BF16) for i in range(2)]; t_Eb = [Trk(), Trk()]
    gl = SC.r1(12288, 4096).rearrange("p (a c) -> p a c", a=8)[0:64]; t_gl = SC.rt(12288, 4096) + SC.rt(16384, 1024)
    ldt = sb("ldt", [64, 2]); t_ldt = Trk()
    pl = sb("pl", [128, 8, 64]); t_pl = Trk()
    bA = SC.r1(0, 4096).rearrange("p (g i) -> p g i", g=64); bB = SC.r1(4096, 4096).rearrange("p (g i) -> p g i", g=64); t_b = SC.rt(0, 12288)
    bbP = sb("bbP", [128, 64, 16], BF16); bbPB = sb("bbPB", [128, 64, 16], BF16); t_bb = Trk()
    cA = sb("cA", [128, 64, 16], BF16); cB = sb("cB", [128, 64, 16], BF16); t_c = Trk()
    PB = [sb("PB0", [128, 8, 128], BF16)]; PB.append(PB[0]); t_PB = [Trk()]; t_PB.append(t_PB[0])
    BpA = sb("BpA", [128, 8, 128], BF16); BpB = sb("BpB", [128, 8, 128], BF16); t_Bp = Trk()
    CpA = sb("CpA", [128, 8, 128], BF16); CpB = sb("CpB", [128, 8, 128], BF16); t_Cp = Trk()
    QST = sb("QST", [128, 64]); t_QST = Trk()
    QCL = sb("QCL", [128, 64]); QSL = sb("QSL", [128, 64]); t_QL = Trk()
    ps = [st.enter_context(nc.psum_tensor("ps%d" % i, [128, NT], F32)) for i in range(8)]
    t_ps = [Trk() for _ in range(8)]

    wplan = [] if record else list(plan)
    wstate = {"next_load": 0, "next_use": 0}

    def wview(d):
        nm, l, r0, rows, c0, cols = d
        return w[nm][l][r0:r0 + rows, c0:c0 + cols].rearrange("(kc p) m -> p kc m", p=128)

    def wget(d):
        i = wstate["next_use"]
        wstate["next_use"] += 1
        kc, cols = d[3] // 128, d[5]
        if record:
            wplan.append(d)
        else:
            assert wplan[i] == d, (i, wplan[i], d)
            while wstate["next_load"] < min(len(wplan), i + NW - 1):
                j = wstate["next_load"]
                dj = wplan[j]
                kj, cj = dj[3] // 128, dj[5]
                dst = wring[j % NW][:, 0:kj * cj].rearrange("p (k c) -> p k c", k=kj)
                S.dma("pool", dst, wview(dj), writes=[t_w[j % NW]], track=False)
                wstate["next_load"] += 1
        return wring[i % NW][:, 0:kc * cols].rearrange("p (k c) -> p k c", k=kc), t_w[i % NW]

    S.dma("sp", ident_f[:], c_ident, writes=[t_const])
    S.dma("sp", ja_f[:], c_ja, writes=[t_const])
    S.dma("sp", jb_f[:], c_jb, writes=[t_const])
    S.dma("sp", iota[:], c_iota, writes=[t_const])
    S.dma("sp", iotas[:], c_iotas, writes=[t_const])
    S.dma("sp", sgn[:], c_sgn, writes=[t_const])
    for i, (nm, l) in enumerate([("norm_mix", 0), ("norm_mix", 1), ("norm_xattn", 0), ("norm_xattn", 1),
                                 ("norm_ffn", 0), ("norm_ffn", 1)]):
        S.dma("sp", gains[:, i, :], w[nm][l].rearrange("(k p) -> p k", p=128), writes=[t_const])
    S.dma("sp", gains[:, 6, :], w["norm_final"].rearrange("(k p) -> p k", p=128), writes=[t_const])
    S.dma("sp", cvec[:, 0, :], w["conv_b_in"][0].rearrange("(k p) -> p k", p=128), writes=[t_const])
    for i, nm in enumerate(["conv_dw_b", "conv_ln_g", "conv_ln_b", "s5_d"]):
        S.dma("sp", cvec[:, 1 + i, 0:8], w[nm][0].rearrange("(k p) -> p k", p=128), writes=[t_const])
    for kc in range(8):
        S.dma("sp", dwt[:, kc, :], w["conv_dw"][0][:, kc * 128:(kc + 1) * 128].rearrange("t p -> p t"), writes=[t_const])
    V(lambda e: e.tensor_copy(out=ident_b[:], in_=ident_f[:]), [t_const], [t_const])
    V(lambda e: e.memset(ones_b[:], 1.0), [], [t_const])
    V(lambda e: e.memset(hist[:], 0.0), [], [t_hist])
    V(lambda e: e.memset(QST[:], 0.0), [], [t_QST])
    V(lambda e: e.memset(PB[0][:], 0.0), [], [t_PB[0]])
    V(lambda e: e.memset(CpA[:], 0.0), [], [t_Cp])
    V(lambda e: e.memset(CpB[:], 0.0), [], [t_Cp])
    epsc = sgn[:, 2:3]
    hpi = sgn[:, 3:4]

    def transpose_to(dst_fn, src_ap, rows, cols, bank, reads, writes, eng="act"):
        PE(lambda e: e.transpose(out=ps[bank][0:cols, 0:rows], in_=src_ap, identity=ident_f[0:rows, 0:rows]),
           reads + [t_const], [t_ps[bank]])
        f = dst_fn(ps[bank][0:cols, 0:rows])
        (A if eng == "act" else V)(f, [t_ps[bank]], writes)

    a_re, a_im = w["s5_a_re"][0], w["s5_a_im"][0]
    for h in range(2):
        S.dma("sp", gl[:, 0, 64 * h:64 * h + 64], a_re, writes=[t_gl])
        S.dma("sp", gl[:, 1, 64 * h:64 * h + 64], a_im, writes=[t_gl])
    S.dma("sp", ldt[:, 0:1], w["s5_log_dt"][0].rearrange("(g o) -> g o", o=1), writes=[t_ldt])
    A(lambda e: e.activation(out=ldt[:, 1:2], in_=ldt[:, 0:1], func=AF.Exp), [t_ldt], [t_ldt])
    dtc = ldt[:, 1:2]
    G2 = lambda i: gl[:, i, :]
    gops = [t_gl, t_ldt, t_const]
    A(lambda e: e.activation(out=G2(2), in_=G2(0), func=AF.Exp, scale=dtc), gops, [t_gl])
    V(lambda e: e.tensor_scalar(out=G2(3), in0=G2(1), scalar1=dtc, scalar2=None, op0=ALU.mult), gops, [t_gl])
    V(lambda e: e.tensor_scalar(out=G2(4), in0=G2(3), scalar1=1.0 / TWO_PI, scalar2=MAGIC, op0=ALU.mult, op1=ALU.add), gops, [t_gl])
    V(lambda e: e.tensor_scalar(out=G2(4), in0=G2(4), scalar1=-MAGIC, scalar2=None, op0=ALU.add), gops, [t_gl])
    V(lambda e: e.scalar_tensor_tensor(out=G2(5), in0=G2(4), scalar=-C1, in1=G2(3), op0=ALU.mult, op1=ALU.add), gops, [t_gl])
    V(lambda e: e.scalar_tensor_tensor(out=G2(4), in0=G2(4), scalar=-C2, in1=G2(5), op0=ALU.mult, op1=ALU.add), gops, [t_gl])
    V(lambda e: e.tensor_scalar(out=G2(4), in0=G2(4), scalar1=-math.pi, scalar2=math.pi, op0=ALU.max, op1=ALU.min), gops, [t_gl])
    A(lambda e: e.activation(out=G2(5), in_=G2(4), func=AF.Abs), gops, [t_gl])
    A(lambda e: e.activation(out=G2(4), in_=G2(4), func=AF.Sin), gops, [t_gl])
    A(lambda e: e.activation(out=G2(5), in_=G2(5), func=AF.Sin, scale=-1.0, bias=hpi[0:64, :]), gops, [t_gl])
    V(lambda e: e.tensor_tensor(out=G2(4), in0=G2(4), in1=G2(2), op=ALU.mult), gops, [t_gl])
    V(lambda e: e.tensor_tensor(out=G2(5), in0=G2(5), in1=G2(2), op=ALU.mult), gops, [t_gl])
    V(lambda e: e.tensor_scalar(out=G2(5), in0=G2(5), scalar1=-1.0, scalar2=None, op0=ALU.add), gops, [t_gl])
    V(lambda e: e.tensor_tensor(out=G2(6), in0=G2(0), in1=G2(0), op=ALU.mult), gops, [t_gl])
    V(lambda e: e.tensor_tensor(out=G2(7), in0=G2(1), in1=G2(1), op=ALU.mult), gops, [t_gl])
    V(lambda e: e.tensor_tensor(out=G2(6), in0=G2(6), in1=G2(7), op=ALU.add), gops, [t_gl])
    V(lambda e: e.reciprocal(out=G2(6), in_=G2(6)), gops, [t_gl])
    gtmp = SC.r1(16384, 1024).rearrange("p (a c) -> p a c", a=2)[0:64]
    V(lambda e: e.tensor_tensor(out=G2(7), in0=G2(5), in1=G2(0), op=ALU.mult), gops, [t_gl])
    V(lambda e: e.tensor_tensor(out=gtmp[:, 0, :], in0=G2(4), in1=G2(1), op=ALU.mult), gops, [t_gl])
    V(lambda e: e.tensor_tensor(out=G2(7), in0=G2(7), in1=gtmp[:, 0, :], op=ALU.add), gops, [t_gl])
    V(lambda e: e.tensor_tensor(out=G2(7), in0=G2(7), in1=G2(6), op=ALU.mult), gops, [t_gl])
    V(lambda e: e.tensor_tensor(out=gtmp[:, 0, :], in0=G2(4), in1=G2(0), op=ALU.mult), gops, [t_gl])
    V(lambda e: e.tensor_tensor(out=gtmp[:, 1, :], in0=G2(5), in1=G2(1), op=ALU.mult), gops, [t_gl])
    V(lambda e: e.tensor_tensor(out=G2(4), in0=gtmp[:, 0, :], in1=gtmp[:, 1, :], op=ALU.subtract), gops, [t_gl])
    V(lambda e: e.tensor_tensor(out=G2(4), in0=G2(4), in1=G2(6), op=ALU.mult), gops, [t_gl])
    for dsti, srci in ((0, 3), (2, 2), (3, 7), (6, 4)):
        transpose_to(lambda p, dsti=dsti: (lambda e: e.activation(out=pl[:, dsti, :], in_=p, func=AF.Copy)),
                     gl[:, srci, :], 64, 128, 7, [t_gl], [t_pl])
    V(lambda e: e.tensor_scalar(out=pl[:, 1, :], in0=pl[:, 0, :], scalar1=1.0 / TWO_PI, scalar2=None, op0=ALU.mult), [t_pl], [t_pl])
    V(lambda e: e.tensor_scalar(out=pl[:, 4, :], in0=pl[:, 6, :], scalar1=sgn[:, 1:2], scalar2=None, op0=ALU.mult), [t_pl, t_const], [t_pl])
    V(lambda e: e.tensor_scalar(out=pl[:, 5, :], in0=pl[:, 3, :], scalar1=sgn[:, 0:1], scalar2=None, op0=ALU.mult), [t_pl, t_const], [t_pl])
    b_re, b_im = w["s5_b_re"][0], w["s5_b_im"][0]
    for (dst, lo, hi) in ((bA, b_re, b_im), (bB, b_im, b_re)):
        S.dma("sp", dst[0:64], lo.rearrange("g p i -> p g i"), writes=[t_b])
        S.dma("sp", dst[64:128], hi.rearrange("g p i -> p g i"), writes=[t_b])
    bc = lambda i: pl[:, i, :].unsqueeze(2).to_broadcast([128, 64, 16])
    tmpbb = SC.r1(8192, 4096).rearrange("p (g i) -> p g i", g=64)
    tmpbb2 = SC.r1(12288, 4096).rearrange("p (g i) -> p g i", g=64)
    t_t2 = SC.rt(12288, 4096)
    V(lambda e: e.tensor_tensor(out=tmpbb2[:], in0=bA[:], in1=bc(3), op=ALU.mult), [t_b, t_pl], [t_t2])
    V(lambda e: e.tensor_tensor(out=tmpbb[:], in0=bB[:], in1=bc(4), op=ALU.mult), [t_b, t_pl], [t_b])
    V(lambda e: e.tensor_tensor(out=bbP[:], in0=tmpbb2[:], in1=tmpbb[:], op=ALU.add), [t_b, t_t2], [t_bb])
    V(lambda e: e.tensor_tensor(out=tmpbb2[:], in0=bB[:], in1=bc(5), op=ALU.mult), [t_b, t_pl], [t_t2])
    V(lambda e: e.tensor_tensor(out=tmpbb[:], in0=bA[:], in1=bc(6), op=ALU.mult), [t_b, t_pl], [t_b])
    V(lambda e: e.tensor_tensor(out=bbPB[:], in0=tmpbb2[:], in1=tmpbb[:], op=ALU.add), [t_b, t_t2], [t_bb])
    c_re = w["s5_c_re"][0].rearrange("g j p -> (g j) p")
    c_im = w["s5_c_im"][0].rearrange("g j p -> (g j) p")
    for r in range(8):
        sgt = stage[r % 2]
        S.dma("sp", sgt[:, 0:64], c_re[r * 128:(r + 1) * 128, :], writes=[t_stage[r % 2]])
        S.dma("sp", sgt[:, 64:128], c_im[r * 128:(r + 1) * 128, :], writes=[t_stage[r % 2]])
        S.dma("sp", sgt[:, 128:192], c_im[r * 128:(r + 1) * 128, :], writes=[t_stage[r % 2]])
        S.dma("sp", sgt[:, 192:256], c_re[r * 128:(r + 1) * 128, :], writes=[t_stage[r % 2]])
        dA = cA[:, 8 * r:8 * r + 8, :].rearrange("p g j -> p (g j)")
        dB = cB[:, 8 * r:8 * r + 8, :].rearrange("p g j -> p (g j)")
        transpose_to(lambda p, dA=dA: (lambda e: e.tensor_scalar(out=dA, in0=p, scalar1=sgn[:, 0:1], scalar2=None, op0=ALU.mult)),
                     sgt[:, 0:128], 128, 128, 6, [t_stage[r % 2]], [t_c], eng="dve")
        transpose_to(lambda p, dB=dB: (lambda e: e.tensor_scalar(out=dB, in0=p, scalar1=-1.0, scalar2=None, op0=ALU.mult)),
                     sgt[:, 128:256], 128, 128, 7, [t_stage[r % 2]], [t_c], eng="dve")

    def rmsnorm(C, dst, t_dst, gi, N, f32out=False):
        for kc in range(8):
            A(lambda e, kc=kc: e.activation(out=C.sq[:, kc, 0:N], in_=C.xres[:, kc, 0:N], func=AF.Square), [C.t_xres], [C.t_sq])
        for kc in range(8):
            PE(lambda e, kc=kc: e.matmul(ps[4][:, 0:N], lhsT=ones_b[:], rhs=C.sq[:, kc, 0:N], start=(kc == 0), stop=(kc == 7)),
               [C.t_sq, t_const], [t_ps[4]])
        A(lambda e: e.activation(out=tmpf[0][:, 0:N], in_=ps[4][:, 0:N], func=AF.Sqrt, scale=1.0 / D, bias=epsc), [t_ps[4], t_const], [t_tmpf[0]])
        V(lambda e: e.reciprocal(out=tmpf[0][:, 0:N], in_=tmpf[0][:, 0:N]), [t_tmpf[0]], [t_tmpf[0]])
        for kc in range(8):
            V(lambda e, kc=kc: e.scalar_tensor_tensor(out=dst[:, kc, 0:N], in0=C.xres[:, kc, 0:N], scalar=gains[:, gi, kc:kc + 1],
                                                    in1=tmpf[0][:, 0:N], op0=ALU.mult, op1=ALU.mult),
              [C.t_xres, t_tmpf[0], t_const], [t_dst])

    def cost(x):
        return float(x)

    class Lag:
        def __init__(self):
            self.p = None

        def push(self, fn):
            if self.p is not None:
                self.p()
            self.p = fn

        def flush(self):
            if self.p is not None:
                self.p()
            self.p = None

    bank_rr = {"i": 0}

    def next_bank():
        b = bank_rr["i"]
        bank_rr["i"] = (b + 1) % 4
        return b

    def dense_chunk(src, t_src, KC, wslot, t_wslot, mi, N, bank):
        for kc in range(KC):
            PE(lambda e, kc=kc: e.matmul(ps[bank][:, 0:N], lhsT=wslot[:, kc, mi * 128:(mi + 1) * 128], rhs=src[:, kc, 0:N],
                                         start=(kc == 0), stop=(kc == KC - 1)), [t_src, t_wslot], [t_ps[bank]])

    def dense_resid(C, src, t_src, N, wname, l):
        lag = Lag()
        for blk in range(2):
            ws, tw = wget((wname, l, 0, 1024, blk * 512, 512))
            for mi in range(4):
                b = next_bank()
                dense_chunk(src, t_src, 8, ws, tw, mi, N, b)
                m = blk * 4 + mi
                lag.push(lambda m=m, b=b: V(lambda e: e.tensor_tensor(out=C.xres[:, m, 0:N], in0=ps[b][:, 0:N], in1=C.xres[:, m, 0:N], op=ALU.add),
                                            [t_ps[b], C.t_xres], [C.t_xres]))
            lag.flush()
            yield cost(8)

    def glu_dense(src, t_src, N, bias, consume, wname):
        lag = Lag()
        for half in range(2):
            wa, twa = wget((wname, 0, 0, 1024, half * 512, 512))
            wg, twg = wget((wname, 0, 0, 1024, 1024 + half * 512, 512))
            for mi in range(4):
                m = half * 4 + mi
                ba, bg = next_bank(), next_bank()
                dense_chunk(src, t_src, 8, wa, twa, mi, N, ba)
                dense_chunk(src, t_src, 8, wg, twg, mi, N, bg)
                ti = 1 + (m % 2)

                def ev(m=m, ba=ba, bg=bg, ti=ti):
                    if bias:
                        A(lambda e: e.activation(out=tmpf[ti][:, 0:N], in_=ps[bg][:, 0:N], func=AF.Sigmoid, bias=cvec[:, 0, 8 + m:9 + m]),
                          [t_ps[bg], t_const], [t_tmpf[ti]])
                    else:
                        A(lambda e: e.activation(out=tmpf[ti][:, 0:N], in_=ps[bg][:, 0:N], func=AF.Sigmoid), [t_ps[bg]], [t_tmpf[ti]])
                    consume(m, ba, ti)
                lag.push(ev)
            lag.flush()
            yield cost(9)

    for r in range(2):
        S.dma("sp", stage[r][:], mem[r * 128:(r + 1) * 128, :], writes=[t_stage[r]])
        for kc in range(8):
            transpose_to(lambda p, kc=kc, r=r: (lambda e: e.activation(out=memT[:, kc, r * 128:(r + 1) * 128], in_=p, func=AF.Copy)),
                         stage[r][:, kc * 128:(kc + 1) * 128], 128, 128, 6 + (kc % 2), [t_stage[r]], [t_memT])
    for l in range(2):
        for which in range(2):
            for nb in range(2):
                ws, tw = wget(("xattn_w_k" if which == 0 else "xattn_w_v", l, 0, 1024, nb * 512, 512))
                for mt in range(2):
                    b = next_bank()
                    for kc in range(8):
                        PE(lambda e, kc=kc, mt=mt, b=b, ws=ws: e.matmul(ps[b][:], lhsT=memT[:, kc, mt * 128:(mt + 1) * 128], rhs=ws[:, kc, :],
                                                                    start=(kc == 0), stop=(kc == 7)), [t_memT, tw], [t_ps[b]])
                    si = mt
                    A(lambda e, b=b, si=si: e.activation(out=stage[si][:, 0:512], in_=ps[b][:], func=AF.Copy), [t_ps[b]], [t_stage[si]])
                    dst = (mkp if which == 0 else mvp)[l, mt * 128:(mt + 1) * 128, nb * 512:(nb + 1) * 512]
                    out_toks.append(S.dma("sp", dst, stage[si][:, 0:512], reads=[t_stage[si]]))
                    if which == 1:
                        V(lambda e, l=l, mt=mt, nb=nb, si=si: e.tensor_copy(out=Vb[l][:, mt, nb * 512:(nb + 1) * 512], in_=stage[si][:, 0:512]),
                          [t_stage[si]], [t_Vb[l]])
                if which == 0:
                    for fi in range(4):
                        b = next_bank()
                        for kc in range(8):
                            PE(lambda e, kc=kc, fi=fi, b=b, ws=ws: e.matmul(ps[b][:, 0:256], lhsT=ws[:, kc, fi * 128:(fi + 1) * 128], rhs=memT[:, kc, :],
                                                                        start=(kc == 0), stop=(kc == 7)), [t_memT, tw], [t_ps[b]])
                        V(lambda e, l=l, fc=nb * 4 + fi, b=b: e.tensor_copy(out=KT[l][:, fc, :], in_=ps[b][:, 0:256]), [t_ps[b]], [t_KT[l]])

    def run_tile(ti, C):
        sample = (ti == 4)
        N = 64 if sample else NT
        t0 = ti * NT
        if not sample:
            for r in range(4):
                sgt, tsg = stage[r % 2], t_stage[r % 2]
                S.dma("sp", sgt[:], xp[t0 + r * 128:t0 + (r + 1) * 128, :], writes=[tsg])
                for kc in range(8):
                    transpose_to(lambda p, kc=kc, r=r: (lambda e: e.activation(out=C.xres[:, kc, r * 128:(r + 1) * 128], in_=p, func=AF.Copy)),
                                 sgt[:, kc * 128:(kc + 1) * 128], 128, 128, 6 + (kc % 2), [tsg], [C.t_xres])
                yield cost(4)
        else:
            S.dma("sp", stage[0][0:64, :], xs, writes=[t_stage[0]])
            for kc in range(8):
                transpose_to(lambda p, kc=kc: (lambda e: e.activation(out=C.xres[:, kc, 0:64], in_=p, func=AF.Copy)),
                             stage[0][0:64, kc * 128:(kc + 1) * 128], 64, 128, 6 + (kc % 2), [t_stage[0]], [C.t_xres])
            yield cost(3)

        def dump():
            if DEBUG and ti == 0:
                out_toks.append(S.dma("sp", dbg[dbg_n[0]], C.xres[:], reads=[C.t_xres]))
                dbg_n[0] += 1
        dump()
        for l in range(2):
            rmsnorm(C, C.xn, C.t_xn, l, N)
            yield cost(6)
            if l == 0:
                yield "need:conv"
                yield from conv_mixer(C, ti, sample, N)
            else:
                yield "need:s5"
                yield "done:in_s5"
                yield from s5_mixer(C, ti, sample, N)
                yield "done:s5"
            dump()
            rmsnorm(C, C.xn, C.t_xn, 2 + l, N)
            qlag = Lag()
            for blk in range(2):
                ws, tw = wget(("xattn_w_q", l, 0, 1024, blk * 512, 512))
                for mi in range(4):
                    b = next_bank()
                    dense_chunk(C.xn, C.t_xn, 8, ws, tw, mi, N, b)
                    qlag.push(lambda m=blk * 4 + mi, b=b: A(lambda e: e.activation(out=C.actb[:, m, 0:N], in_=ps[b][:, 0:N], func=AF.Identity, scale=1.0 / 16.0),
                                                           [t_ps[b]], [C.t_actb]))
                qlag.flush()
                yield cost(8)
            if not sample:
                yield from attn_prompt(C, l, N)
            else:
                yield from attn_sample(C, l)
            yield from dense_resid(C, C.xn, C.t_xn, N, "xattn_w_o", l)
            dump()
            rmsnorm(C, C.xn, C.t_xn, 4 + l, N)
            yield cost(6)
            mlag = Lag()
            for half in range(2):
                for blk in range(4):
                    ws, tw = wget(("mlp_w_up", l, 0, 1024, half * 2048 + blk * 512, 512))
                    for mi in range(4):
                        b = next_bank()
                        dense_chunk(C.xn, C.t_xn, 8, ws, tw, mi, N, b)
                        m = blk * 4 + mi
                        tb = m % 4
                        def evu(m=m, b=b, tb=tb):
                            A(lambda e: e.activation(out=tmpb[tb][:, 0:N], in_=ps[b][:, 0:N], func=AF.Relu), [t_ps[b]], [t_tmpb[tb]])
                            V(lambda e: e.tensor_tensor(out=C.hid[:, m, 0:N], in0=tmpb[tb][:, 0:N], in1=tmpb[tb][:, 0:N], op=ALU.mult),
                              [t_tmpb[tb]], [C.t_hid])
                        mlag.push(evu)
                    mlag.flush()
                    yield cost(8)
                for m in range(8):
                    ws, tw = wget(("mlp_w_down", l, half * 2048, 2048, m * 128, 128))
                    b = next_bank()
                    dense_chunk(C.hid, C.t_hid, 16, ws, tw, 0, N, b)
                    mlag.push(lambda m=m, b=b: V(lambda e: e.tensor_tensor(out=C.xres[:, m, 0:N], in0=ps[b][:, 0:N], in1=C.xres[:, m, 0:N], op=ALU.add),
                                                 [t_ps[b], C.t_xres], [C.t_xres]))
                    if m % 2 == 1:
                        mlag.flush()
                        yield cost(8)
        dump()
        rmsnorm(C, C.cf, C.t_cf, 6, N, f32out=True)
        yield cost(6)
        if not sample:
            for r in range(4):
                sgt, tsg = stage[r % 2], t_stage[r % 2]
                for kc in range(8):
                    bk = 6 + (kc % 2)
                    PE(lambda e, kc=kc, r=r, bk=bk: e.transpose(out=ps[bk][:, 0:128], in_=C.cf[:, kc, r * 128:(r + 1) * 128], identity=ident_f[:]),
                       [C.t_cf, t_const], [t_ps[bk]])
                    A(lambda e, kc=kc, bk=bk, sgt=sgt: e.activation(out=sgt[:, kc * 128:(kc + 1) * 128], in_=ps[bk][:, 0:128], func=AF.Copy), [t_ps[bk]], [tsg])
                out_toks.append(S.dma("sp", yp[t0 + r * 128:t0 + (r + 1) * 128, :], sgt[:], reads=[tsg]))
                yield cost(4)
        else:
            for kc in range(8):
                bk = 6 + (kc % 2)
                PE(lambda e, kc=kc, bk=bk: e.transpose(out=ps[bk][0:64, 0:128], in_=C.cf[:, kc, 0:64], identity=ident_f[:]),
                   [C.t_cf, t_const], [t_ps[bk]])
                A(lambda e, kc=kc, bk=bk: e.activation(out=stage[0][0:64, kc * 128:(kc + 1) * 128], in_=ps[bk][0:64, 0:128], func=AF.Copy), [t_ps[bk]], [t_stage[0]])
            out_toks.append(S.dma("sp", ys, stage[0][0:64, :], reads=[t_stage[0]]))
            yield cost(3)

    def conv_mixer(C, ti, sample, N):
        if sample:
            for r in range(4):
                sgt, tsg = stage[r % 2], t_stage[r % 2]
                S.dma("sp", sgt[0:120, :], cc[4 * r:4 * r + 4].rearrange("b j c -> (b j) c"), writes=[tsg])
                for kc in range(8):
                    transpose_to(lambda p, kc=kc, r=r: (lambda e: e.activation(
                        out=C.gpads[:, kc, 4 * r:4 * r + 4, 0:30], in_=p.rearrange("p (b j) -> p b j", b=4), func=AF.Copy)),
                        sgt[0:120, kc * 128:(kc + 1) * 128], 120, 128, 6 + (kc % 2), [tsg], [C.t_gpads])
                out_toks.append(S.dma("sp", convs[4 * r:4 * r + 4, 0:26, :], cc[4 * r:4 * r + 4, 4:30, :]))
        last = (ti == 3)
        if not sample:
            V(lambda e: e.tensor_copy(out=C.gpad[:, :, 0:30], in_=hist[:]), [t_hist], [C.t_gpad])

        def consume(m, ba, tix):
            if sample:
                dstb = C.gpads[:, m, :, 30:34]
                srcp = ps[ba][:, 0:64].rearrange("p (b l) -> p b l", b=16)
                sg = tmpf[tix][:, 0:64].rearrange("p (b l) -> p b l", b=16)
                V(lambda e: e.scalar_tensor_tensor(out=dstb, in0=srcp, scalar=cvec[:, 0, m:m + 1], in1=sg, op0=ALU.add, op1=ALU.mult),
                  [t_ps[ba], t_tmpf[tix], t_const], [C.t_gpads])
                V(lambda e: e.scalar_tensor_tensor(out=gl30[:, m, 0:64], in0=ps[ba][:, 0:64], scalar=cvec[:, 0, m:m + 1], in1=tmpf[tix][:, 0:64],
                                                   op0=ALU.add, op1=ALU.mult), [t_ps[ba], t_tmpf[tix], t_const], [t_gl30])
            else:
                V(lambda e: e.scalar_tensor_tensor(out=C.gpad[:, m, 30:30 + N], in0=ps[ba][:, 0:N], scalar=cvec[:, 0, m:m + 1], in1=tmpf[tix][:, 0:N],
                                                   op0=ALU.add, op1=ALU.mult), [t_ps[ba], t_tmpf[tix], t_const], [C.t_gpad])
                if last:
                    V(lambda e: e.scalar_tensor_tensor(out=gl30[:, m, 0:30], in0=ps[ba][:, N - 30:N], scalar=cvec[:, 0, m:m + 1],
                                                       in1=tmpf[tix][:, N - 30:N], op0=ALU.add, op1=ALU.mult),
                      [t_ps[ba], t_tmpf[tix], t_const], [t_gl30])

        yield from glu_dense(C.xn, C.t_xn, N, True, consume, "conv_w_in")
        if sample:
            for kc in range(8):
                transpose_to(lambda p, kc=kc: (lambda e: e.activation(out=stage[1][0:64, kc * 128:(kc + 1) * 128], in_=p, func=AF.Copy)),
                             gl30[:, kc, 0:64], 128, 64, 6 + (kc % 2), [t_gl30], [t_stage[1]])
            for b_ in range(16):
                out_toks.append(S.dma("sp", convs[b_, 26:30, :], stage[1][4 * b_:4 * b_ + 4, :], reads=[t_stage[1]]))
        elif last:
            for kc in range(8):
                transpose_to(lambda p, kc=kc: (lambda e: e.activation(out=stage[1][0:30, kc * 128:(kc + 1) * 128], in_=p, func=AF.Copy)),
                             gl30[:, kc, 0:30], 128, 30, 6 + (kc % 2), [t_gl30], [t_stage[1]])
            out_toks.append(S.dma("sp", convp, stage[1][0:30, :], reads=[t_stage[1]]))
        dcnt = 0
        for kc in range(8):
            b = next_bank()
            for k in range(31):
                di_ = dcnt % 4
                dcnt += 1
                if k % 2 == 0:
                    V(lambda e, kc=kc, k=k, di_=di_: e.tensor_scalar(out=diag[di_][:], in0=ident_b[:], scalar1=dwt[:, kc, k:k + 1], scalar2=None, op0=ALU.mult),
                      [t_const], [t_diag[di_]])
                else:
                    A(lambda e, kc=kc, k=k, di_=di_: e.activation(out=diag[di_][:], in_=ident_b[:], func=AF.Identity, scale=dwt[:, kc, k:k + 1]),
                      [t_const], [t_diag[di_]])
                if sample:
                    PE(lambda e, kc=kc, k=k, di_=di_, b=b: e.matmul(ps[b][:, 0:64].rearrange("p (b l) -> p b l", b=16), lhsT=diag[di_][:],
                                                                    rhs=C.gpads[:, kc, :, k:k + 4], start=(k == 0), stop=(k == 30)),
                       [t_diag[di_], C.t_gpads], [t_ps[b]])
                else:
                    PE(lambda e, kc=kc, k=k, di_=di_, b=b: e.matmul(ps[b][:, 0:N], lhsT=diag[di_][:], rhs=C.gpad[:, kc, k:k + N],
                                                                    start=(k == 0), stop=(k == 30)), [t_diag[di_], C.t_gpad], [t_ps[b]])
            A(lambda e, kc=kc, b=b: e.activation(out=C.cf[:, kc, 0:N], in_=ps[b][:, 0:N], func=AF.Identity, bias=cvec[:, 1, kc:kc + 1]),
              [t_ps[b], t_const], [C.t_cf])
            if kc % 2 == 1:
                yield cost(15)
        if not sample:
            V(lambda e: e.tensor_copy(out=hist[:], in_=C.gpad[:, :, N:N + 30]), [C.t_gpad], [t_hist])
        yield "done:conv"
        for kc in range(8):
            i0, i1 = 2 * (kc % 2), 2 * (kc % 2) + 1
            V(lambda e, kc=kc, i0=i0: e.tensor_copy(out=tmpb[i0][:, 0:N], in_=C.cf[:, kc, 0:N]), [C.t_cf], [t_tmpb[i0]])
            A(lambda e, kc=kc, i1=i1: e.activation(out=tmpb[i1][:, 0:N], in_=C.cf[:, kc, 0:N], func=AF.Square), [C.t_cf], [t_tmpb[i1]])
            PE(lambda e, kc=kc, i0=i0: e.matmul(ps[4][:, 0:N], lhsT=ones_b[:], rhs=tmpb[i0][:, 0:N], start=(kc == 0), stop=(kc == 7)),
               [t_tmpb[i0], t_const], [t_ps[4]])
            PE(lambda e, kc=kc, i1=i1: e.matmul(ps[3][:, 0:N], lhsT=ones_b[:], rhs=tmpb[i1][:, 0:N], start=(kc == 0), stop=(kc == 7)),
               [t_tmpb[i1], t_const], [t_ps[3]])
        mu, ms, rs = tmpf[1], tmpf[2], tmpf[3]
        A(lambda e: e.activation(out=mu[:, 0:N], in_=ps[4][:, 0:N], func=AF.Identity, scale=1.0 / D), [t_ps[4]], [t_tmpf[1]])
        V(lambda e: e.tensor_tensor(out=ms[:, 0:N], in0=mu[:, 0:N], in1=mu[:, 0:N], op=ALU.mult), [t_tmpf[1]], [t_tmpf[2]])
        V(lambda e: e.scalar_tensor_tensor(out=ms[:, 0:N], in0=ps[3][:, 0:N], scalar=1.0 / D, in1=ms[:, 0:N], op0=ALU.mult, op1=ALU.subtract),
          [t_ps[3], t_tmpf[2]], [t_tmpf[2]])
        A(lambda e: e.activation(out=rs[:, 0:N], in_=ms[:, 0:N], func=AF.Sqrt, bias=epsc), [t_tmpf[2], t_const], [t_tmpf[3]])
        V(lambda e: e.reciprocal(out=rs[:, 0:N], in_=rs[:, 0:N]), [t_tmpf[3]], [t_tmpf[3]])
        for kc in range(8):
            V(lambda e, kc=kc: e.tensor_tensor(out=C.cf[:, kc, 0:N], in0=C.cf[:, kc, 0:N], in1=mu[:, 0:N], op=ALU.subtract), [C.t_cf, t_tmpf[1]], [C.t_cf])
            V(lambda e, kc=kc: e.tensor_tensor(out=C.cf[:, kc, 0:N], in0=C.cf[:, kc, 0:N], in1=rs[:, 0:N], op=ALU.mult), [C.t_cf, t_tmpf[3]], [C.t_cf])
            A(lambda e, kc=kc: e.activation(out=C.xn[:, kc, 0:N], in_=C.cf[:, kc, 0:N], func=AF.Silu, scale=cvec[:, 2, kc:kc + 1], bias=cvec[:, 3, kc:kc + 1]),
              [C.t_cf, t_const], [C.t_xn])
        yield cost(8)
        yield from dense_resid(C, C.xn, C.t_xn, N, "conv_w_out", 0)

    def attn_prompt(C, l, N):
        for h in range(4):
            for mc in range(2):
                for ec in range(2):
                    PE(lambda e, mc=mc, ec=ec, h=h: e.matmul(ps[mc][:, 0:N], lhsT=KT[l][:, 2 * h + ec, mc * 128:(mc + 1) * 128], rhs=C.actb[:, 2 * h + ec, 0:N],
                                                     start=(ec == 0), stop=(ec == 1)), [t_KT[l], C.t_actb], [t_ps[mc]])
                A(lambda e, mc=mc: e.activation(out=Eb[mc][:, 0:N], in_=ps[mc][:, 0:N], func=AF.Exp), [t_ps[mc]], [t_Eb[mc]])
            for mc in range(2):
                PE(lambda e, mc=mc: e.matmul(ps[2][:, 0:N], lhsT=ones_b[:], rhs=Eb[mc][:, 0:N], start=(mc == 0), stop=(mc == 1)),
                   [t_Eb[mc], t_const], [t_ps[2]])
            V(lambda e: e.reciprocal(out=tmpf[1][:, 0:N], in_=ps[2][:, 0:N]), [t_ps[2]], [t_tmpf[1]])
            for ec in range(2):
                bk = 3 + ec
                for mc in range(2):
                    PE(lambda e, mc=mc, ec=ec, bk=bk, h=h: e.matmul(ps[bk][:, 0:N], lhsT=Vb[l][:, mc, (2 * h + ec) * 128:(2 * h + ec + 1) * 128], rhs=Eb[mc][:, 0:N],
                                                             start=(mc == 0), stop=(mc == 1)), [t_Vb[l], t_Eb[mc]], [t_ps[bk]])
                V(lambda e, ec=ec, bk=bk, h=h: e.tensor_tensor(out=C.xn[:, 2 * h + ec, 0:N], in0=ps[bk][:, 0:N], in1=tmpf[1][:, 0:N], op=ALU.mult),
                  [t_ps[bk], t_tmpf[1]], [C.t_xn])
            yield cost(6)

    def attn_sample(C, l):
        sc_v = ps[0][:].rearrange("p (b h m l) -> p b h m l", b=16, h=4, m=2)
        first = True
        for b_ in range(16):
            i = 0
            S.dma("pool", C.Kc[i][:], cmk[l, b_].rearrange("(mt p) c -> p mt c", p=128), writes=[C.t_Kc[i]])
            for mt in range(2):
                for fq in range(2):
                    bk = 6 + fq
                    pv = ps[bk][:].bitcast(BF16)
                    for f4 in range(4):
                        fc = fq * 4 + f4
                        PE(lambda e, mt=mt, fc=fc, f4=f4, pv=pv, i=i: e.transpose(out=pv[:, f4 * 128:(f4 + 1) * 128], in_=C.Kc[i][:, mt, fc * 128:(fc + 1) * 128],
                                                                              identity=ident_b[:]), [C.t_Kc[i], t_const], [t_ps[bk]])
                    V(lambda e, mt=mt, fq=fq, pv=pv, i=i: e.tensor_copy(out=C.KTs[i][:, fq * 4:fq * 4 + 4, mt * 128:(mt + 1) * 128],
                                                                       in_=pv[:, 0:512].rearrange("p (f m) -> p f m", f=4)), [t_ps[bk]], [C.t_KTs[i]])
            for h in range(4):
                for mc in range(2):
                    for ec in range(2):
                        fst = first
                        first = False
                        PE(lambda e, h=h, mc=mc, ec=ec, fst=fst, i=i, b_=b_: e.matmul(sc_v[:, b_, h, mc, :], lhsT=C.KTs[i][:, 2 * h + ec, mc * 128:(mc + 1) * 128],
                                                                                 rhs=C.actb[:, 2 * h + ec, 4 * b_:4 * b_ + 4], start=fst, stop=True, skip_group_check=True),
                           [C.t_KTs[i], C.t_actb], [t_ps[0]])
        A(lambda e: e.activation(out=Eb[0][:], in_=ps[0][:], func=AF.Exp), [t_ps[0]], [t_Eb[0]])
        Ev = Eb[0][:].rearrange("p (b h m l) -> p b h m l", b=16, h=4, m=2)
        Ev2 = Eb[0][:].rearrange("p (bh m l) -> p bh m l", bh=64, m=2)
        for mc in range(2):
            PE(lambda e, mc=mc: e.matmul(ps[1][:, 0:256].rearrange("p (bh l) -> p bh l", bh=64), lhsT=ones_b[:], rhs=Ev2[:, :, mc, :],
                                         start=(mc == 0), stop=(mc == 1)), [t_Eb[0], t_const], [t_ps[1]])
        V(lambda e: e.reciprocal(out=tmpf[1][:, 0:256], in_=ps[1][:, 0:256]), [t_ps[1]], [t_tmpf[1]])
        ov = ps[2][:].rearrange("p (f b l) -> p f b l", f=8, b=16)
        first = True
        for b_ in range(16):
            i = 0
            S.dma("pool", C.Vc[i][:], cmv[l, b_].rearrange("(mt p) c -> p mt c", p=128), writes=[C.t_Vc[i]])
            for fc in range(8):
                for mc in range(2):
                    fst = first
                    first = False
                    PE(lambda e, fc=fc, mc=mc, fst=fst, i=i, b_=b_: e.matmul(ov[:, fc, b_, :], lhsT=C.Vc[i][:, mc, fc * 128:(fc + 1) * 128], rhs=Ev[:, b_, fc // 2, mc, :],
                                                                        start=fst, stop=True, skip_group_check=True), [C.t_Vc[i], t_Eb[0]], [t_ps[2]])
        rv = tmpf[1][:, 0:256].rearrange("p (b h l) -> p b h l", b=16, h=4)
        for fc in range(8):
            V(lambda e, fc=fc: e.tensor_tensor(out=C.xn[:, fc, 0:64].rearrange("p (b l) -> p b l", b=16), in0=ov[:, fc, :, :], in1=rv[:, :, fc // 2, :], op=ALU.mult),
              [t_ps[2], t_tmpf[1]], [C.t_xn])
        yield cost(60)

    def s5_prompt_loop(C, jit, tables, xaxb, data, ymm, N):
        tables(0)
        for kc in range(8):
            jit(kc)
            for k in range(8):
                g = 8 * kc + k
                xaxb(g)
                if g + 1 < 64:
                    tables(g + 1)
                data(g)
                if k > 0:
                    ymm(g - 1)
                yield cost(7)
            ymm(8 * kc + 7)
            tf = tmpf[1 + (kc % 2)]
            ttf = t_tmpf[1 + (kc % 2)]
            V(lambda e, kc=kc, tf=tf: e.scalar_tensor_tensor(out=tf[:, 0:N], in0=C.xn[:, kc, 0:N], scalar=cvec[:, 4, kc:kc + 1], in1=ps[5][:, 0:N],
                                                           op0=ALU.mult, op1=ALU.add), [C.t_xn, t_ps[5], t_const], [ttf])
            A(lambda e, kc=kc, tf=tf: e.activation(out=C.actb[:, kc, 0:N], in_=tf[:, 0:N], func=AF.Gelu_apprx_tanh), [ttf], [C.t_actb])
            yield cost(3)

    def s5_sample_batched(C, jit):
        r1, rt = C.r1, C.rt
        SINs = r1(0, 1024).rearrange("p (g l) -> p g l", g=64); tS = rt(0, 1024)
        COSs = r1(1024, 1024).rearrange("p (g l) -> p g l", g=64); tCo = rt(1024, 1024)
        UA = r1(2048, 1024).rearrange("p (g l) -> p g l", g=64); tUA = rt(2048, 1024)
        KK = r1(3072, 1024).rearrange("p (g l) -> p g l", g=64); tKK = rt(3072, 1024)
        F = r1(20480, 2048).rearrange("p (k b l) -> p k b l", k=8, b=16); tF = rt(20480, 2048)
        Gq = r1(22528, 2048).rearrange("p (k b l) -> p k b l", k=8, b=16); tG = rt(22528, 2048)
        QC = r1(18432, 1024, BF16); QS = r1(19456, 1024, BF16); tQ = rt(18432, 2048)
        TMP = r1(3072, 512).rearrange("p (k b) -> p k b", k=8)
        th = pl[:, 1, :].unsqueeze(2).to_broadcast([128, 64, 4])
        i4 = iotas[:, 0:4].unsqueeze(1).to_broadcast([128, 64, 4])
        V(lambda e: e.tensor_tensor(out=UA, in0=th, in1=i4, op=ALU.mult), [t_pl, t_const], [tUA])
        V(lambda e: e.tensor_scalar(out=KK, in0=UA, scalar1=MAGIC, scalar2=-MAGIC, op0=ALU.add, op1=ALU.add), [tUA], [tKK])
        V(lambda e: e.tensor_tensor(out=KK, in0=KK, in1=UA, op=ALU.subtract), [tUA, tKK], [tKK])
        A(lambda e: e.activation(out=SINs, in_=KK, func=AF.Sin, scale=-SC2PI), [tKK], [tS])
        A(lambda e: e.activation(out=UA, in_=KK, func=AF.Abs), [tKK], [tUA])
        A(lambda e: e.activation(out=COSs, in_=UA, func=AF.Sin, scale=-SC2PI, bias=hpi), [tUA, t_const], [tCo])
        yield cost(4)
        for kc in range(8):
            jit(kc)
            gs = slice(8 * kc, 8 * kc + 8)
            for k in range(8):
                PE(lambda e, k=k, kc=kc: e.matmul(ps[2][:, k * 64:(k + 1) * 64], lhsT=BpA[:, k, :], rhs=C.xn[:, kc, 0:64], start=True, stop=True, skip_group_check=True),
                   [t_Bp, C.t_xn], [t_ps[2]])
                PE(lambda e, k=k, kc=kc: e.matmul(ps[3][:, k * 64:(k + 1) * 64], lhsT=BpB[:, k, :], rhs=C.xn[:, kc, 0:64], start=True, stop=True, skip_group_check=True),
                   [t_Bp, C.t_xn], [t_ps[3]])
            cb = COSs[:, gs, :].unsqueeze(2).to_broadcast([128, 8, 16, 4])
            sb_ = SINs[:, gs, :].unsqueeze(2).to_broadcast([128, 8, 16, 4])
            xa = ps[2][:].rearrange("p (k b l) -> p k b l", k=8, b=16)
            xb = ps[3][:].rearrange("p (k b l) -> p k b l", k=8, b=16)
            V(lambda e, xa=xa, cb=cb: e.tensor_tensor(out=F, in0=xa, in1=cb, op=ALU.mult), [t_ps[2], tCo], [tF])
            V(lambda e, xb=xb, sb_=sb_: e.tensor_tensor(out=Gq, in0=xb, in1=sb_, op=ALU.mult), [t_ps[3], tS], [tG])
            V(lambda e: e.tensor_tensor(out=F, in0=F, in1=Gq, op=ALU.add), [tF, tG], [tF])
            rb = pl[:, 2, gs].unsqueeze(2).to_broadcast([128, 8, 16])
            for l_ in range(4):
                prev = C.H0[:, gs, :] if l_ == 0 else Gq[:, :, :, l_ - 1]
                V(lambda e, prev=prev, rb=rb: e.tensor_tensor(out=TMP, in0=prev, in1=rb, op=ALU.mult), [C.t_H0, tG, t_pl], [tKK])
                V(lambda e, l_=l_: e.tensor_tensor(out=Gq[:, :, :, l_], in0=TMP, in1=F[:, :, :, l_], op=ALU.add), [tKK, tF], [tG])
            V(lambda e, cb=cb: e.tensor_tensor(out=QC.rearrange("p (k b l) -> p k b l", k=8, b=16), in0=Gq, in1=cb, op=ALU.mult), [tG, tCo], [tQ])
            V(lambda e, sb_=sb_: e.tensor_tensor(out=QS.rearrange("p (k b l) -> p k b l", k=8, b=16), in0=Gq, in1=sb_, op=ALU.mult), [tG, tS], [tQ])
            c3 = COSs[:, gs, 3:4].to_broadcast([128, 8, 16])
            s3 = SINs[:, gs, 3:4].to_broadcast([128, 8, 16])
            V(lambda e, gs=gs, c3=c3: e.tensor_tensor(out=C.QCLs[:, gs, :], in0=Gq[:, :, :, 3], in1=c3, op=ALU.mult), [tG, tCo], [C.t_QLs])
            V(lambda e, gs=gs, s3=s3: e.tensor_tensor(out=C.QSLs[:, gs, :], in0=Gq[:, :, :, 3], in1=s3, op=ALU.mult), [tG, tS], [C.t_QLs])
            for k in range(8):
                PE(lambda e, k=k: e.matmul(ps[5][:, 0:64], lhsT=CpA[:, k, :], rhs=QC[:, k * 64:(k + 1) * 64], start=(k == 0), stop=False), [t_Cp, tQ], [t_ps[5]])
                PE(lambda e, k=k: e.matmul(ps[5][:, 0:64], lhsT=CpB[:, k, :], rhs=QS[:, k * 64:(k + 1) * 64], start=False, stop=(k == 7)), [t_Cp, tQ], [t_ps[5]])
            tf = tmpf[1 + (kc % 2)]
            ttf = t_tmpf[1 + (kc % 2)]
            V(lambda e, kc=kc, tf=tf: e.scalar_tensor_tensor(out=tf[:, 0:64], in0=C.xn[:, kc, 0:64], scalar=cvec[:, 4, kc:kc + 1], in1=ps[5][:, 0:64],
                                                           op0=ALU.mult, op1=ALU.add), [C.t_xn, t_ps[5], t_const], [ttf])
            A(lambda e, kc=kc, tf=tf: e.activation(out=C.actb[:, kc, 0:64], in_=tf[:, 0:64], func=AF.Gelu_apprx_tanh), [ttf], [C.t_actb])
            yield cost(10)

    def s5_mixer(C, ti, sample, N):
        last = (ti == 3)
        io = iotas[:, 0:64] if sample else iota[:, 0:NT]
        t0f = 0.0 if sample else float(ti * NT)
        V(lambda e: e.tensor_scalar(out=C.bU[:], in0=pl[:, 1, :], scalar1=t0f, scalar2=None, op0=ALU.mult), [t_pl], [C.t_bU])
        V(lambda e: e.tensor_scalar(out=tmpf[3][:, 0:64], in0=C.bU[:], scalar1=MAGIC, scalar2=-MAGIC, op0=ALU.add, op1=ALU.add), [C.t_bU], [t_tmpf[3]])
        V(lambda e: e.tensor_tensor(out=C.bU[:], in0=C.bU[:], in1=tmpf[3][:, 0:64], op=ALU.subtract), [C.t_bU, t_tmpf[3]], [C.t_bU])
        if sample:
            for r in range(8):
                S.dma("sp", C.HBr[:, :, 0:64], sre[:, r * 512:(r + 1) * 512].rearrange("b (g p) -> b g p", g=8), writes=[C.t_HBr])
                S.dma("sp", C.HBr[:, :, 64:128], sim[:, r * 512:(r + 1) * 512].rearrange("b (g p) -> b g p", g=8), writes=[C.t_HBr])
                for g_ in range(8):
                    PE(lambda e, g_=g_: e.transpose(out=ps[6][:, g_ * 16:(g_ + 1) * 16], in_=C.HBr[:, g_, :], identity=ident_f[0:16, 0:16]),
                       [C.t_HBr, t_const], [t_ps[6]])
                A(lambda e, r=r: e.activation(out=C.H0[:, 8 * r:8 * r + 8, :], in_=ps[6][:, 0:128].rearrange("p (g b) -> p g b", g=8), func=AF.Copy),
                  [t_ps[6]], [C.t_H0])
            yield cost(6)
        W = C.s5ws if sample else C.s5w
        TW = C.t_s5ws if sample else C.t_s5w
        ta_, tb_ = TW[0], TW[1]
        a_, b_ = W[0], W[1]

        def tables(g):
            si = g % 2
            d_, e_ = W[2 + 2 * si], W[3 + 2 * si]
            td_, te_ = TW[2 + 2 * si], TW[3 + 2 * si]
            A(lambda e: e.activation(out=a_[:, 0:N], in_=io, func=AF.Identity, scale=pl[:, 1, g:g + 1], bias=C.bU[:, g:g + 1]),
              [t_pl, t_const, C.t_bU], [ta_])
            A(lambda e: e.activation(out=b_[:, 0:N], in_=a_[:, 0:N], func=AF.Identity, bias=sgn_magic()), [ta_, t_const], [tb_])
            V(lambda e: e.scalar_tensor_tensor(out=b_[:, 0:N], in0=b_[:, 0:N], scalar=-MAGIC, in1=a_[:, 0:N], op0=ALU.add, op1=ALU.subtract),
              [ta_, tb_], [tb_])
            A(lambda e: e.activation(out=d_[:, 0:N], in_=b_[:, 0:N], func=AF.Sin, scale=-SC2PI), [tb_], [td_])
            A(lambda e: e.activation(out=a_[:, 0:N], in_=b_[:, 0:N], func=AF.Abs), [tb_], [ta_])
            A(lambda e: e.activation(out=e_[:, 0:N], in_=a_[:, 0:N], func=AF.Sin, scale=-SC2PI, bias=hpi), [ta_, t_const], [te_])

        def jit(kc):
            for (srcb, dstb, bank) in ((bbP, BpA, 6), (bbPB, BpB, 7)):
                V(lambda e, srcb=srcb: e.tensor_copy(out=_diag(PB[0]), in_=srcb[:, 8 * kc:8 * kc + 8, :]), [t_bb], [t_PB[0]])
                for k in range(8):
                    PE(lambda e, k=k, bank=bank: e.transpose(out=psqb(bank, k), in_=PB[0][:, k, :], identity=ident_b[:]), [t_PB[0], t_const], [t_ps[bank]])
                    if k % 4 == 3:
                        k0 = k - 3
                        A(lambda e, k0=k0, bank=bank, dstb=dstb: e.activation(out=dstb[:, k0:k0 + 4, :], in_=ps[bank][:].bitcast(BF16)[:, 0:512].rearrange("p (k c) -> p k c", k=4),
                                                                        func=AF.Copy), [t_ps[bank]], [t_Bp])
            V(lambda e: e.tensor_copy(out=_diag(CpA), in_=cA[:, 8 * kc:8 * kc + 8, :]), [t_c], [t_Cp])
            V(lambda e: e.tensor_copy(out=_diag(CpB), in_=cB[:, 8 * kc:8 * kc + 8, :]), [t_c], [t_Cp])

        def data(g):
            kc, k = g // 8, g % 8
            si = g % 2
            d_, e_, f_, q_ = W[2 + 2 * si], W[3 + 2 * si], W[6 + 2 * si], W[7 + 2 * si]
            td_, te_, tf_, tq_ = TW[2 + 2 * si], TW[3 + 2 * si], TW[6 + 2 * si], TW[7 + 2 * si]
            qc_, qs_, tqc_ = C.qcb[si], C.qsb[si], C.t_qcs[si]
            ba, bb_ = 2 * si, 2 * si + 1
            yb = 5
            V(lambda e: e.tensor_tensor(out=f_[:, 0:N], in0=ps[ba][:, 0:N], in1=e_[:, 0:N], op=ALU.mult), [t_ps[ba], te_], [tf_])
            V(lambda e: e.tensor_tensor(out=q_[:, 0:N], in0=ps[bb_][:, 0:N], in1=d_[:, 0:N], op=ALU.mult), [t_ps[bb_], td_], [tq_])
            V(lambda e: e.tensor_tensor(out=f_[:, 0:N], in0=f_[:, 0:N], in1=q_[:, 0:N], op=ALU.add), [tf_, tq_], [tf_])
            rcol = pl[:, 2, g:g + 1]
            if not sample:
                V(lambda e: e.tensor_tensor_scan(out=q_[:, 0:N], data0=rcol.to_broadcast([128, N]), data1=f_[:, 0:N],
                                                 initial=QST[:, g:g + 1], op0=ALU.mult, op1=ALU.add), [tf_, t_pl, t_QST], [tq_])
                A(lambda e: e.activation(out=QST[:, g:g + 1], in_=q_[:, N - 1:N], func=AF.Copy), [tq_], [t_QST])
            else:
                zv = f_[:, 0:64].rearrange("p (b l) -> p b l", b=16)
                qv = q_[:, 0:64].rearrange("p (b l) -> p b l", b=16)
                for l_ in range(4):
                    prev = C.H0[:, g, :] if l_ == 0 else qv[:, :, l_ - 1]
                    V(lambda e, l_=l_, prev=prev: e.scalar_tensor_tensor(out=qv[:, :, l_], in0=prev, scalar=rcol, in1=zv[:, :, l_],
                                                                      op0=ALU.mult, op1=ALU.add), [tf_, t_pl, C.t_H0, tq_], [tq_])
            V(lambda e: e.tensor_tensor(out=qc_[:, 0:N], in0=q_[:, 0:N], in1=e_[:, 0:N], op=ALU.mult), [tq_, te_], [tqc_])
            V(lambda e: e.tensor_tensor(out=qs_[:, 0:N], in0=q_[:, 0:N], in1=d_[:, 0:N], op=ALU.mult), [tq_, td_], [tqc_])
            if last:
                V(lambda e: e.tensor_tensor(out=QCL[:, g:g + 1], in0=q_[:, N - 1:N], in1=e_[:, N - 1:N], op=ALU.mult), [tq_, te_], [t_QL])
                V(lambda e: e.tensor_tensor(out=QSL[:, g:g + 1], in0=q_[:, N - 1:N], in1=d_[:, N - 1:N], op=ALU.mult), [tq_, td_], [t_QL])
            if sample:
                q3 = q_[:, 0:64].rearrange("p (b l) -> p b l", b=16)[:, :, 3]
                c3 = e_[:, 0:64].rearrange("p (b l) -> p b l", b=16)[:, :, 3]
                s3 = d_[:, 0:64].rearrange("p (b l) -> p b l", b=16)[:, :, 3]
                V(lambda e: e.tensor_tensor(out=C.QCLs[:, g, :], in0=q3, in1=c3, op=ALU.mult), [tq_, te_], [C.t_QLs])
                V(lambda e: e.tensor_tensor(out=C.QSLs[:, g, :], in0=q3, in1=s3, op=ALU.mult), [tq_, td_], [C.t_QLs])

        def xaxb(g):
            kc, k = g // 8, g % 8
            si = g % 2
            ba, bb_ = 2 * si, 2 * si + 1
            PE(lambda e: e.matmul(ps[ba][:, 0:N], lhsT=BpA[:, k, :], rhs=C.xn[:, kc, 0:N], start=True, stop=True), [t_Bp, C.t_xn], [t_ps[ba]])
            PE(lambda e: e.matmul(ps[bb_][:, 0:N], lhsT=BpB[:, k, :], rhs=C.xn[:, kc, 0:N], start=True, stop=True), [t_Bp, C.t_xn], [t_ps[bb_]])

        def ymm(g):
            k = g % 8
            si = g % 2
            qc_, qs_, tqc_ = C.qcb[si], C.qsb[si], C.t_qcs[si]
            yb = 5
            PE(lambda e: e.matmul(ps[yb][:, 0:N], lhsT=CpA[:, k, :], rhs=qc_[:, 0:N], start=(k == 0), stop=False), [t_Cp, tqc_], [t_ps[yb]])
            PE(lambda e: e.matmul(ps[yb][:, 0:N], lhsT=CpB[:, k, :], rhs=qs_[:, 0:N], start=False, stop=(k == 7)), [t_Cp, tqc_], [t_ps[yb]])

        if sample:
            yield from s5_sample_batched(C, jit)
        else:
            yield from s5_prompt_loop(C, jit, tables, xaxb, data, ymm, N)

        def consume(m, ba, tix):
            V(lambda e: e.tensor_tensor(out=tmpf[tix][:, 0:N], in0=ps[ba][:, 0:N], in1=tmpf[tix][:, 0:N], op=ALU.mult), [t_ps[ba], t_tmpf[tix]], [t_tmpf[tix]])
            V(lambda e: e.tensor_tensor(out=C.xres[:, m, 0:N], in0=C.xres[:, m, 0:N], in1=tmpf[tix][:, 0:N], op=ALU.add), [t_tmpf[tix], C.t_xres], [C.t_xres])

        yield from glu_dense(C.actb, C.t_actb, N, False, consume, "s5_w_glu")
        if last:
            PE(lambda e: e.matmul(ps[6][:, 0:64], lhsT=ja_f[:], rhs=QCL[:], start=True, stop=False), [t_QL, t_const], [t_ps[6]])
            PE(lambda e: e.matmul(ps[6][:, 0:64], lhsT=jb_f[:], rhs=QSL[:], start=False, stop=True), [t_QL, t_const], [t_ps[6]])
            A(lambda e: e.activation(out=tmpf[3][:, 0:64], in_=ps[6][:, 0:64], func=AF.Copy), [t_ps[6]], [t_tmpf[3]])
            PE(lambda e: e.transpose(out=ps[7][0:64, 0:128], in_=tmpf[3][:, 0:64], identity=ident_f[:]), [t_tmpf[3], t_const], [t_ps[7]])
            A(lambda e: e.activation(out=stage[1][0:64, 0:128], in_=ps[7][0:64, 0:128], func=AF.Copy), [t_ps[7]], [t_stage[1]])
            out_toks.append(S.dma("sp", s5rp, stage[1][0:64, 0:64], reads=[t_stage[1]]))
            out_toks.append(S.dma("sp", s5ip, stage[1][0:64, 64:128], reads=[t_stage[1]]))
        if sample:
            for hh in range(2):
                PE(lambda e, hh=hh: e.matmul(ps[hh][:], lhsT=ja_f[:], rhs=C.QCLs[:, 32 * hh:32 * hh + 32, :].rearrange("p g b -> p (g b)"), start=True, stop=False),
                   [C.t_QLs, t_const], [t_ps[hh]])
                PE(lambda e, hh=hh: e.matmul(ps[hh][:], lhsT=jb_f[:], rhs=C.QSLs[:, 32 * hh:32 * hh + 32, :].rearrange("p g b -> p (g b)"), start=False, stop=True),
                   [C.t_QLs, t_const], [t_ps[hh]])
                A(lambda e, hh=hh: e.activation(out=C.H0[:, 32 * hh:32 * hh + 32, :].rearrange("p g b -> p (g b)"), in_=ps[hh][:], func=AF.Copy), [t_ps[hh]], [C.t_H0])
            s5rs_v = s5rs.rearrange("b (g p) -> b g p", g=64)
            s5is_v = s5is.rearrange("b (g p) -> b g p", g=64)
            for r in range(16):
                for g4 in range(4):
                    g = 4 * r + g4
                    PE(lambda e, g=g, g4=g4: e.transpose(out=ps[6][0:16, g4 * 128:(g4 + 1) * 128], in_=C.H0[:, g, :], identity=ident_f[:]), [C.t_H0, t_const], [t_ps[6]])
                A(lambda e: e.activation(out=C.osm[:], in_=ps[6][0:16, :].rearrange("p (g c) -> p g c", g=4), func=AF.Copy), [t_ps[6]], [C.t_osm])
                out_toks.append(S.dma("sp", s5rs_v[:, 4 * r:4 * r + 4, :], C.osm[:, :, 0:64], reads=[C.t_osm]))
                out_toks.append(S.dma("sp", s5is_v[:, 4 * r:4 * r + 4, :], C.osm[:, :, 64:128], reads=[C.t_osm]))

    def psqb(bank, k):
        return ps[bank][:].bitcast(BF16)[:, (k % 4) * 128:(k % 4 + 1) * 128]

    def psq(bank, k):
        return ps[bank][:, (k % 4) * 128:(k % 4 + 1) * 128]

    def _diag(t):
        flat = t[:].rearrange("p k c -> p (k c)")
        return bass.AP(tensor=flat.tensor, offset=flat.offset, ap=[list(flat.ap[0]), [144, 8], [1, 16]])

    magic_t = sb("magic_t", [128, 1])
    V(lambda e: e.memset(magic_t[:], MAGIC), [], [t_const])

    def sgn_magic():
        return magic_t[:, 0:1]

    order = [0, 4, 1, 2, 3]
    gens = [run_tile(order[p], ctxs[p % 2]) for p in range(5)]
    acc = [0.0] * 5
    fin = [False] * 5
    marks = [set() for _ in range(5)]
    blocked = [None] * 5
    active = []
    started = 0
    while True:
        while started < 5 and len(active) < 2 and (started < 2 or fin[started - 2]) and (started <= 1 or "in_s5" in marks[started - 1] or fin[started - 1]):
            acc[started] = max([acc[t] for t in active], default=0.0)
            active.append(started)
            started += 1
        if not active:
            break
        for t in active:
            if blocked[t] is not None:
                key = blocked[t]
                if t == 0 or fin[t - 1] or key in marks[t - 1]:
                    blocked[t] = None
                    acc[t] = max(acc[t], max([acc[x] for x in active if x != t], default=0.0))
        cands = [t for t in active if blocked[t] is None]
        assert cands, (active, blocked)
        t = min(cands, key=lambda x: acc[x])
        try:
            r = next(gens[t])
        except StopIteration:
            fin[t] = True
            active.remove(t)
            continue
        if isinstance(r, str):
            kind, key = r.split(":")
            if kind == "done":
                marks[t].add(key)
            else:
                blocked[t] = key
        else:
            acc[t] += r
    S.wait_all("sp", out_toks)
    if record:
        st.close()
        return None, wplan
    S.run(st)
    st.close()
    return nc, wplan


_CACHE = {}


def kernel(**inp):
    f = lambda a: np.ascontiguousarray(np.asarray(a, dtype=np.float32))
    if "nc" not in _CACHE:
        _, plan = build_nc(None)
        _CACHE["nc"], _ = build_nc(plan)
    nc = _CACHE["nc"]
    ident = np.eye(128, dtype=np.float32)
    iota = np.broadcast_to(np.arange(1, 513, dtype=np.float32), (128, 512)).copy()
    iotas = np.broadcast_to(np.tile(np.arange(1, 5, dtype=np.float32), 16), (128, 64)).copy()
    sgn = np.zeros((128, 4), np.float32)
    sgn[:64, 0], sgn[64:, 0] = 1.0, -1.0
    sgn[:64, 1], sgn[64:, 1] = -1.0, 1.0
    sgn[:, 2] = EPS
    sgn[:, 3] = math.pi / 2
    ja = np.eye(128, dtype=np.float32)
    jb = np.zeros((128, 128), np.float32)
    for p in range(64):
        jb[64 + p, p] = -1.0
        jb[p, 64 + p] = 1.0
    shared = {n: f(inp[n]) for n, _ in W_IN}
    shared.update(c_ident=ident, c_iota=iota, c_iotas=iotas, c_sgn=sgn, c_ja=ja, c_jb=jb)
    in_maps = []
    for c in range(8):
        m = dict(shared)
        m["xp"] = f(inp["x_prompt"][c])
        m["xs"] = f(inp["x_sample"][16 * c:16 * c + 16]).reshape(64, D)
        m["mem"] = f(inp["mem_prompt"][c])
        m["cc"] = f(inp["cache_conv"][0, 16 * c:16 * c + 16])
        m["sre"] = f(inp["state_s5_re"][0, 16 * c:16 * c + 16]).reshape(16, 4096)
        m["sim"] = f(inp["state_s5_im"][0, 16 * c:16 * c + 16]).reshape(16, 4096)
        m["cmk"] = f(inp["cache_mem_k"][:, 16 * c:16 * c + 16]).reshape(2, 16, 256, D)
        m["cmv"] = f(inp["cache_mem_v"][:, 16 * c:16 * c + 16]).reshape(2, 16, 256, D)
        in_maps.append(m)
    res = run_bass_kernel_spmd(nc, in_maps, core_ids=list(range(8))).results
    if DEBUG:
        _CACHE["dbg"] = np.asarray(res[0]["dbg"])
    cat = lambda k: np.stack([np.asarray(r[k]) for r in res])
    y_prompt = cat("yp")
    y_sample = cat("ys").reshape(128, 4, D)
    conv_prompt = cat("convp")[None]
    conv_sample = cat("convs").reshape(1, 128, 30, D)
    s5_re_p, s5_im_p = cat("s5rp")[None], cat("s5ip")[None]
    s5_re_s = cat("s5rs").reshape(1, 128, 64, 64)
    s5_im_s = cat("s5is").reshape(1, 128, 64, 64)
    mk = cat("mkp").transpose(1, 0, 2, 3).reshape(2, 8, 256, 4, 256)
    mv = cat("mvp").transpose(1, 0, 2, 3).reshape(2, 8, 256, 4, 256)
    return (y_prompt, y_sample, conv_prompt, conv_sample, s5_re_p, s5_im_p, s5_re_s, s5_im_s, mk, mv)
```

```python
import contextlib
import math
import numpy as np
import concourse.bass as bass
import concourse.mybir as mybir
from concourse.bass_utils import run_bass_kernel_spmd

F32 = mybir.dt.float32
BF16 = mybir.dt.bfloat16
AF = mybir.ActivationFunctionType
ALU = mybir.AluOpType

N_DMA_SEMS = 24


class Trk:
    __slots__ = ("last_w", "readers")

    def __init__(self):
        self.last_w = None
        self.readers = []


def _flat(x):
    out = []
    for t in x:
        if isinstance(t, (list, tuple)):
            out.extend(_flat(t))
        else:
            out.append(t)
    return out


class Q:
    def __init__(self, name):
        self.name = name
        self.ins = []
        self.waited = {}
        self.sem = None


class Sched:
    def __init__(self, nc):
        self.nc = nc
        self.q = {n: Q(n) for n in ("pe", "act", "dve", "pool", "sp")}
        self.dma_cnt = [0] * N_DMA_SEMS
        self.dma_rr2 = {0: 0, N_DMA_SEMS // 2: 0}
        self.open_dmas = []

    def _deps(self, reads, writes):
        deps = []
        for t in reads:
            if t.last_w is not None:
                deps.append((t.last_w, "raw"))
        for t in writes:
            if t.last_w is not None:
                deps.append((t.last_w, "waw"))
            for r in t.readers:
                deps.append((r, "war"))
        return deps

    def _add_waits(self, q, deps):
        waits = []
        for tok, kind in deps:
            if tok[0] == "e":
                _, qn, idx = tok
                if qn == q.name and qn == "pe":
                    continue
                key = ("e", qn)
                if q.waited.get(key, -1) >= idx:
                    continue
                q.waited[key] = idx
                self.q[qn].ins[idx][2] = True
                waits.append(tok)
            else:
                _, k, v = tok
                key = ("d", k)
                if q.waited.get(key, -1) >= v:
                    continue
                q.waited[key] = v
                waits.append(tok)
        return waits

    def _update(self, tok, reads, writes):
        for t in writes:
            t.last_w = tok
            t.readers = []
        for t in reads:
            if t in writes:
                continue
            if tok[0] == "e":
                t.readers = [r for r in t.readers if not (r[0] == "e" and r[1] == tok[1])]
            t.readers.append(tok)

    def emit(self, qn, fn, reads=(), writes=()):
        reads, writes = _flat(reads), _flat(writes)
        q = self.q[qn]
        waits = self._add_waits(q, self._deps(reads, writes))
        idx = len(q.ins)
        q.ins.append([fn, waits, False, None])
        tok = ("e", qn, idx)
        self._update(tok, reads, writes)
        return tok

    def barrier(self):
        lasts = {}
        for qn, q in self.q.items():
            for idx in range(len(q.ins) - 1, -1, -1):
                if q.ins[idx][0] is not None and q.ins[idx][3] is None:
                    lasts[qn] = idx
                    break
        for qn, q in self.q.items():
            deps = [(("e", q2n, idx), "raw") for q2n, idx in lasts.items() if q2n != qn]
            deps += [(t, "raw") for t in self.open_dmas]
            waits = self._add_waits(q, deps)
            q.ins.append([None, waits, False, None])
        self.open_dmas = []

    def dma(self, qn, out, in_, reads=(), writes=(), track=True):
        reads, writes = _flat(reads), _flat(writes)
        q = self.q[qn]
        half = N_DMA_SEMS // 2
        base = half if qn == "pool" else 0
        k = base + self.dma_rr2[base]
        self.dma_rr2[base] = (self.dma_rr2[base] + 1) % half
        self.dma_cnt[k] += 16
        v = self.dma_cnt[k]
        deps = self._deps(reads, writes)
        if v > 16:
            deps.append((("d", k, v - 16), "raw"))
        waits = self._add_waits(q, deps)
        q.ins.append([lambda e: e.dma_start(out=out, in_=in_), waits, False, k])
        tok = ("d", k, v)
        if track:
            self.open_dmas.append(tok)
        self._update(tok, reads, writes)
        return tok

    def wait_all(self, qn, toks):
        q = self.q[qn]
        waits = self._add_waits(q, [(t, "raw") for t in toks])
        q.ins.append([None, waits, False, None])

    def run(self, st):
        nc = self.nc
        for qn, q in self.q.items():
            q.sem = st.enter_context(nc.semaphore("s_" + qn))
        dsems = [st.enter_context(nc.semaphore("d%d" % k)) for k in range(N_DMA_SEMS)]
        for q in self.q.values():
            c = 0
            q.pref = []
            for ins in q.ins:
                if ins[2]:
                    c += 1
                q.pref.append(c)
        st.enter_context(nc.allow_non_contiguous_dma(reason="small one-time parameter loads"))
        block = st.enter_context(nc.Block())

        def body(q):
            def f(e):
                for fn, waits, marked, dk in q.ins:
                    for w in waits:
                        if w[0] == "e":
                            q2 = self.q[w[1]]
                            e.wait_ge(q2.sem, q2.pref[w[2]])
                        else:
                            e.wait_ge(dsems[w[1]], w[2])
                    if fn is None:
                        continue
                    r = fn(e)
                    if dk is not None:
                        r.then_inc(dsems[dk], 16)
                    elif marked:
                        r.then_inc(q.sem, 1)
            return f

        block.sync(body(self.q["sp"]))
        block.tensor(body(self.q["pe"]))
        block.scalar(body(self.q["act"]))
        block.vector(body(self.q["dve"]))
        block.gpsimd(body(self.q["pool"]))


D = 1024
NT = 512
EPS = 1e-6
MAGIC = 12582912.0
TWO_PI = 2.0 * math.pi
C1 = 6.28125
C2 = TWO_PI - C1
SC2PI = TWO_PI * (1.0 - 1.5e-7)

W_IN = [
    ("norm_mix", [2, D]), ("norm_xattn", [2, D]), ("norm_ffn", [2, D]), ("norm_final", [D]),
    ("conv_w_in", [1, D, 2 * D]), ("conv_b_in", [1, 2 * D]), ("conv_dw", [1, 31, D]), ("conv_dw_b", [1, D]),
    ("conv_ln_g", [1, D]), ("conv_ln_b", [1, D]), ("conv_w_out", [1, D, D]),
    ("s5_a_re", [1, 64, 64]), ("s5_a_im", [1, 64, 64]), ("s5_log_dt", [1, 64]),
    ("s5_b_re", [1, 64, 64, 16]), ("s5_b_im", [1, 64, 64, 16]), ("s5_c_re", [1, 64, 16, 64]),
    ("s5_c_im", [1, 64, 16, 64]), ("s5_d", [1, D]), ("s5_w_glu", [1, D, 2 * D]),
    ("xattn_w_q", [2, D, D]), ("xattn_w_k", [2, D, D]), ("xattn_w_v", [2, D, D]), ("xattn_w_o", [2, D, D]),
    ("mlp_w_up", [2, D, 4 * D]), ("mlp_w_down", [2, 4 * D, D]),
]


DEBUG = False
S5_REORDER = 2


def build_nc(plan=None):
    record = plan is None
    nc = bass.Bass("TRN2", target_bir_lowering=False)
    dbg = nc.dram_tensor("dbg", [8, 128, 8, NT], F32, kind="ExternalOutput").ap() if DEBUG else None
    dbg_n = [0]
    di = lambda n, s: nc.dram_tensor(n, s, F32, kind="ExternalInput").ap()
    do = lambda n, s: nc.dram_tensor(n, s, F32, kind="ExternalOutput").ap()
    xp, xs, mem = di("xp", [2048, D]), di("xs", [64, D]), di("mem", [256, D])
    cc, sre, sim = di("cc", [16, 30, D]), di("sre", [16, 4096]), di("sim", [16, 4096])
    cmk, cmv = di("cmk", [2, 16, 256, D]), di("cmv", [2, 16, 256, D])
    w = {n: di(n, s) for n, s in W_IN}
    c_ident, c_iota, c_iotas = di("c_ident", [128, 128]), di("c_iota", [128, 512]), di("c_iotas", [128, 64])
    c_sgn, c_ja, c_jb = di("c_sgn", [128, 4]), di("c_ja", [128, 128]), di("c_jb", [128, 128])
    yp, ys = do("yp", [2048, D]), do("ys", [64, D])
    convp, convs = do("convp", [30, D]), do("convs", [16, 30, D])
    s5rp, s5ip = do("s5rp", [64, 64]), do("s5ip", [64, 64])
    s5rs, s5is = do("s5rs", [16, 4096]), do("s5is", [16, 4096])
    mkp, mvp = do("mkp", [2, 256, D]), do("mvp", [2, 256, D])

    st = contextlib.ExitStack()
    S = Sched(nc)
    out_toks = []

    def sb(name, shape, dt=F32):
        return st.enter_context(nc.sbuf_tensor(name, shape, dt))

    V = lambda fn, r=(), w_=(): S.emit("dve", fn, r, w_)
    A = lambda fn, r=(), w_=(): S.emit("act", fn, r, w_)
    G = lambda fn, r=(), w_=(): S.emit("pool", fn, r, w_)
    PE = lambda fn, r=(), w_=(): S.emit("pe", fn, r, w_)

    R1B = 25600

    class Ctx:
        pass

    def make_ctx(ci):
        C = Ctx()
        C.xres = sb("xres%d" % ci, [128, 8, NT]); C.t_xres = Trk()
        C.xn = sb("xn%d" % ci, [128, 8, NT], BF16); C.t_xn = Trk()
        C.actb = sb("actb%d" % ci, [128, 8, NT], BF16); C.t_actb = Trk()
        R1 = sb("R1_%d" % ci, [128, R1B // 4])
        slots = [Trk() for _ in range(R1B // 1024)]

        def r1(off, nbytes, dt=F32):
            v = R1[:, off // 4:(off + nbytes) // 4]
            return v if dt == F32 else v.bitcast(dt)

        def rt(off, nbytes):
            return slots[off // 1024:(off + nbytes + 1023) // 1024]
        C.r1, C.rt = r1, rt
        C.hid = r1(0, 16384, BF16).rearrange("p (m n) -> p m n", m=16); C.t_hid = rt(0, 16384)
        C.cf = r1(0, 16384).rearrange("p (m n) -> p m n", m=8); C.t_cf = rt(0, 16384)
        C.gpad = r1(16384, 8 * 542 * 2, BF16).rearrange("p (m n) -> p m n", m=8); C.t_gpad = rt(16384, 8704)
        C.gpads = r1(16384, 8 * 16 * 34 * 2, BF16).rearrange("p (m b n) -> p m b n", m=8, b=16); C.t_gpads = C.t_gpad
        C.sq = r1(16384, 8192, BF16).rearrange("p (m n) -> p m n", m=8); C.t_sq = rt(16384, 8192)
        C.bU = sb("bU%d" % ci, [128, 64]); C.t_bU = Trk()
        C.Kc = [r1(0, 4096, BF16).rearrange("p (m c) -> p m c", m=2)]; C.t_Kc = [rt(0, 4096)]
        C.Vc = [r1(4096, 4096, BF16).rearrange("p (m c) -> p m c", m=2)]; C.t_Vc = [rt(4096, 4096)]
        C.KTs = [r1(8192, 4096, BF16).rearrange("p (f m) -> p f m", f=8)]; C.t_KTs = [rt(8192, 4096)]
        C.s5w = [r1(j * 2048, 2048) for j in range(10)]; C.t_s5w = [rt(j * 2048, 2048) for j in range(10)]
        C.qcb = [r1(20480 + 2048 * i, 1024, BF16) for i in range(2)]; C.qsb = [r1(21504 + 2048 * i, 1024, BF16) for i in range(2)]
        C.t_qcs = [rt(20480 + 2048 * i, 2048) for i in range(2)]
        C.s5ws = [r1(j * 256, 256) for j in range(10)]; C.t_s5ws = [rt(j * 256, 256) for j in range(10)]
        C.H0 = r1(4096, 4096).rearrange("p (g b) -> p g b", g=64); C.t_H0 = rt(4096, 4096)
        C.QCLs = r1(8192, 4096).rearrange("p (g b) -> p g b", g=64); C.QSLs = r1(12288, 4096).rearrange("p (g b) -> p g b", g=64)
        C.t_QLs = rt(8192, 8192)
        C.HBr = r1(16384, 4096).rearrange("p (g c) -> p g c", g=8)[0:16]; C.t_HBr = rt(16384, 4096)
        C.osm = r1(16384, 2048).rearrange("p (g c) -> p g c", g=4)[0:16]; C.t_osm = rt(16384, 2048)
        return C

    ctxs = [make_ctx(0), make_ctx(1)]
    SC = ctxs[1]
    NW = 3
    wring = [sb("w%d" % i, [128, 4096], BF16) for i in range(NW)]; t_w = [Trk() for _ in range(NW)]
    stage = [sb("stg%d" % i, [128, D]) for i in range(2)]; t_stage = [Trk(), Trk()]
    tmpf = [sb("tmpf%d" % i, [128, NT]) for i in range(4)]; t_tmpf = [Trk() for _ in range(4)]
    tmpb = [sb("tmpb%d" % i, [128, NT], BF16) for i in range(4)]; t_tmpb = [Trk() for _ in range(4)]
    ident_f = sb("ident_f", [128, 128]); ident_b = sb("ident_b", [128, 128], BF16); ones_b = sb("ones_b", [128, 128], BF16)
    ja_f = sb("ja_f", [128, 128]); jb_f = sb("jb_f", [128, 128])
    iota = sb("iota", [128, 512]); iotas = sb("iotas", [128, 64]); sgn = sb("sgn", [128, 4])
    t_const = Trk()
    gains = sb("gains", [128, 7, 8])
    cvec = sb("cvec", [128, 5, 16])
    dwt = sb("dwt", [128, 8, 31])
    diag = [sb("diag%d" % i, [128, 128], BF16) for i in range(4)]; t_diag = [Trk() for _ in range(4)]
    hist = sb("hist", [128, 8, 30], BF16); t_hist = Trk()
    gl30 = sb("gl30", [128, 8, 64]); t_gl30 = Trk()
    memT = SC.r1(20480, 4096, BF16).rearrange("p (k m) -> p k m", k=8); t_memT = SC.rt(20480, 4096)
    KT = [sb("KT%d" % l, [128, 8, 256], BF16) for l in range(2)]; t_KT = [Trk(), Trk()]
    Vb = [sb("Vb%d" % l, [128, 2, D], BF16) for l in range(2)]; t_Vb = [Trk(), Trk()]
    Eb = [sb("Eb%d" % i, [128, NT], BF16) for i in range(2)]; t_Eb = [Trk(), Trk()]
    gl = SC.r1(12288, 4096).rearrange("p (a c) -> p a c", a=8)[0:64]; t_gl = SC.rt(12288, 4096) + SC.rt(16384, 1024)
    ldt = sb("ldt", [64, 2]); t_ldt = Trk()
    pl = sb("pl", [128, 8, 64]); t_pl = Trk()
    bA = SC.r1(0, 4096).rearrange("p (g i) -> p g i", g=64); bB = SC.r1(4096, 4096).rearrange("p (g i) -> p g i", g=64); t_b = SC.rt(0, 12288)
    bbP = sb("bbP", [128, 64, 16], BF16); bbPB = sb("bbPB", [128, 64, 16], BF16); t_bb = Trk()
    cA = sb("cA", [128, 64, 16], BF16); cB = sb("cB", [128, 64, 16], BF16); t_c = Trk()
    PB = [sb("PB0", [128, 8, 128], BF16)]; PB.append(PB[0]); t_PB = [Trk()]; t_PB.append(t_PB[0])
    BpA = sb("BpA", [128, 8, 128], BF16); BpB = sb("BpB", [128, 8, 128], BF16); t_Bp = Trk()
    CpA = sb("CpA", [128, 8, 128], BF16); CpB = sb("CpB", [128, 8, 128], BF16); t_Cp = Trk()
    QST = sb("QST", [128, 64]); t_QST = Trk()
    QCL = sb("QCL", [128, 64]); QSL = sb("QSL", [128, 64]); t_QL = Trk()
    ps = [st.enter_context(nc.psum_tensor("ps%d" % i, [128, NT], F32)) for i in range(8)]
    t_ps = [Trk() for _ in range(8)]

    wplan = [] if record else list(plan)
    wstate = {"next_load": 0, "next_use": 0}

    def wview(d):
        nm, l, r0, rows, c0, cols = d
        return w[nm][l][r0:r0 + rows, c0:c0 + cols].rearrange("(kc p) m -> p kc m", p=128)

    def wget(d):
        i = wstate["next_use"]
        wstate["next_use"] += 1
        kc, cols = d[3] // 128, d[5]
        if record:
            wplan.append(d)
        else:
            assert wplan[i] == d, (i, wplan[i], d)
            while wstate["next_load"] < min(len(wplan), i + NW - 1):
                j = wstate["next_load"]
                dj = wplan[j]
                kj, cj = dj[3] // 128, dj[5]
                dst = wring[j % NW][:, 0:kj * cj].rearrange("p (k c) -> p k c", k=kj)
                S.dma("pool", dst, wview(dj), writes=[t_w[j % NW]], track=False)
                wstate["next_load"] += 1
        return wring[i % NW][:, 0:kc * cols].rearrange("p (k c) -> p k c", k=kc), t_w[i % NW]

    S.dma("sp", ident_f[:], c_ident, writes=[t_const])
    S.dma("sp", ja_f[:], c_ja, writes=[t_const])
    S.dma("sp", jb_f[:], c_jb, writes=[t_const])
    S.dma("sp", iota[:], c_iota, writes=[t_const])
    S.dma("sp", iotas[:], c_iotas, writes=[t_const])
    S.dma("sp", sgn[:], c_sgn, writes=[t_const])
    for i, (nm, l) in enumerate([("norm_mix", 0), ("norm_mix", 1), ("norm_xattn", 0), ("norm_xattn", 1),
                                 ("norm_ffn", 0), ("norm_ffn", 1)]):
        S.dma("sp", gains[:, i, :], w[nm][l].rearrange("(k p) -> p k", p=128), writes=[t_const])
    S.dma("sp", gains[:, 6, :], w["norm_final"].rearrange("(k p) -> p k", p=128), writes=[t_const])
    S.dma("sp", cvec[:, 0, :], w["conv_b_in"][0].rearrange("(k p) -> p k", p=128), writes=[t_const])
    for i, nm in enumerate(["conv_dw_b", "conv_ln_g", "conv_ln_b", "s5_d"]):
        S.dma("sp", cvec[:, 1 + i, 0:8], w[nm][0].rearrange("(k p) -> p k", p=128), writes=[t_const])
    for kc in range(8):
        S.dma("sp", dwt[:, kc, :], w["conv_dw"][0][:, kc * 128:(kc + 1) * 128].rearrange("t p -> p t"), writes=[t_const])
    V(lambda e: e.tensor_copy(out=ident_b[:], in_=ident_f[:]), [t_const], [t_const])
    V(lambda e: e.memset(ones_b[:], 1.0), [], [t_const])
    V(lambda e: e.memset(hist[:], 0.0), [], [t_hist])
    V(lambda e: e.memset(QST[:], 0.0), [], [t_QST])
    V(lambda e: e.memset(PB[0][:], 0.0), [], [t_PB[0]])
    V(lambda e: e.memset(CpA[:], 0.0), [], [t_Cp])
    V(lambda e: e.memset(CpB[:], 0.0), [], [t_Cp])
    epsc = sgn[:, 2:3]
    hpi = sgn[:, 3:4]

    def transpose_to(dst_fn, src_ap, rows, cols, bank, reads, writes, eng="act"):
        PE(lambda e: e.transpose(out=ps[bank][0:cols, 0:rows], in_=src_ap, identity=ident_f[0:rows, 0:rows]),
           reads + [t_const], [t_ps[bank]])
        f = dst_fn(ps[bank][0:cols, 0:rows])
        (A if eng == "act" else V)(f, [t_ps[bank]], writes)

    a_re, a_im = w["s5_a_re"][0], w["s5_a_im"][0]
    for h in range(2):
        S.dma("sp", gl[:, 0, 64 * h:64 * h + 64], a_re, writes=[t_gl])
        S.dma("sp", gl[:, 1, 64 * h:64 * h + 64], a_im, writes=[t_gl])
    S.dma("sp", ldt[:, 0:1], w["s5_log_dt"][0].rearrange("(g o) -> g o", o=1), writes=[t_ldt])
    A(lambda e: e.activation(out=ldt[:, 1:2], in_=ldt[:, 0:1], func=AF.Exp), [t_ldt], [t_ldt])
    dtc = ldt[:, 1:2]
    G2 = lambda i: gl[:, i, :]
    gops = [t_gl, t_ldt, t_const]
    A(lambda e: e.activation(out=G2(2), in_=G2(0), func=AF.Exp, scale=dtc), gops, [t_gl])
    V(lambda e: e.tensor_scalar(out=G2(3), in0=G2(1), scalar1=dtc, scalar2=None, op0=ALU.mult), gops, [t_gl])
    V(lambda e: e.tensor_scalar(out=G2(4), in0=G2(3), scalar1=1.0 / TWO_PI, scalar2=MAGIC, op0=ALU.mult, op1=ALU.add), gops, [t_gl])
    V(lambda e: e.tensor_scalar(out=G2(4), in0=G2(4), scalar1=-MAGIC, scalar2=None, op0=ALU.add), gops, [t_gl])
    V(lambda e: e.scalar_tensor_tensor(out=G2(5), in0=G2(4), scalar=-C1, in1=G2(3), op0=ALU.mult, op1=ALU.add), gops, [t_gl])
    V(lambda e: e.scalar_tensor_tensor(out=G2(4), in0=G2(4), scalar=-C2, in1=G2(5), op0=ALU.mult, op1=ALU.add), gops, [t_gl])
    V(lambda e: e.tensor_scalar(out=G2(4), in0=G2(4), scalar1=-math.pi, scalar2=math.pi, op0=ALU.max, op1=ALU.min), gops, [t_gl])
    A(lambda e: e.activation(out=G2(5), in_=G2(4), func=AF.Abs), gops, [t_gl])
    A(lambda e: e.activation(out=G2(4), in_=G2(4), func=AF.Sin), gops, [t_gl])
    A(lambda e: e.activation(out=G2(5), in_=G2(5), func=AF.Sin, scale=-1.0, bias=hpi[0:64, :]), gops, [t_gl])
    V(lambda e: e.tensor_tensor(out=G2(4), in0=G2(4), in1=G2(2), op=ALU.mult), gops, [t_gl])
    V(lambda e: e.tensor_tensor(out=G2(5), in0=G2(5), in1=G2(2), op=ALU.mult), gops, [t_gl])
    V(lambda e: e.tensor_scalar(out=G2(5), in0=G2(5), scalar1=-1.0, scalar2=None, op0=ALU.add), gops, [t_gl])
    V(lambda e: e.tensor_tensor(out=G2(6), in0=G2(0), in1=G2(0), op=ALU.mult), gops, [t_gl])
    V(lambda e: e.tensor_tensor(out=G2(7), in0=G2(1), in1=G2(1), op=ALU.mult), gops, [t_gl])
    V(lambda e: e.tensor_tensor(out=G2(6), in0=G2(6), in1=G2(7), op=ALU.add), gops, [t_gl])
    V(lambda e: e.reciprocal(out=G2(6), in_=G2(6)), gops, [t_gl])
    gtmp = SC.r1(16384, 1024).rearrange("p (a c) -> p a c", a=2)[0:64]
    V(lambda e: e.tensor_tensor(out=G2(7), in0=G2(5), in1=G2(0), op=ALU.mult), gops, [t_gl])
    V(lambda e: e.tensor_tensor(out=gtmp[:, 0, :], in0=G2(4), in1=G2(1), op=ALU.mult), gops, [t_gl])
    V(lambda e: e.tensor_tensor(out=G2(7), in0=G2(7), in1=gtmp[:, 0, :], op=ALU.add), gops, [t_gl])
    V(lambda e: e.tensor_tensor(out=G2(7), in0=G2(7), in1=G2(6), op=ALU.mult), gops, [t_gl])
    V(lambda e: e.tensor_tensor(out=gtmp[:, 0, :], in0=G2(4), in1=G2(0), op=ALU.mult), gops, [t_gl])
    V(lambda e: e.tensor_tensor(out=gtmp[:, 1, :], in0=G2(5), in1=G2(1), op=ALU.mult), gops, [t_gl])
    V(lambda e: e.tensor_tensor(out=G2(4), in0=gtmp[:, 0, :], in1=gtmp[:, 1, :], op=ALU.subtract), gops, [t_gl])
    V(lambda e: e.tensor_tensor(out=G2(4), in0=G2(4), in1=G2(6), op=ALU.mult), gops, [t_gl])
    for dsti, srci in ((0, 3), (2, 2), (3, 7), (6, 4)):
        transpose_to(lambda p, dsti=dsti: (lambda e: e.activation(out=pl[:, dsti, :], in_=p, func=AF.Copy)),
                     gl[:, srci, :], 64, 128, 7, [t_gl], [t_pl])
    V(lambda e: e.tensor_scalar(out=pl[:, 1, :], in0=pl[:, 0, :], scalar1=1.0 / TWO_PI, scalar2=None, op0=ALU.mult), [t_pl], [t_pl])
    V(lambda e: e.tensor_scalar(out=pl[:, 4, :], in0=pl[:, 6, :], scalar1=sgn[:, 1:2], scalar2=None, op0=ALU.mult), [t_pl, t_const], [t_pl])
    V(lambda e: e.tensor_scalar(out=pl[:, 5, :], in0=pl[:, 3, :], scalar1=sgn[:, 0:1], scalar2=None, op0=ALU.mult), [t_pl, t_const], [t_pl])
    b_re, b_im = w["s5_b_re"][0], w["s5_b_im"][0]
    for (dst, lo, hi) in ((bA, b_re, b_im), (bB, b_im, b_re)):
        S.dma("sp", dst[0:64], lo.rearrange("g p i -> p g i"), writes=[t_b])
        S.dma("sp", dst[64:128], hi.rearrange("g p i -> p g i"), writes=[t_b])
    bc = lambda i: pl[:, i, :].unsqueeze(2).to_broadcast([128, 64, 16])
    tmpbb = SC.r1(8192, 4096).rearrange("p (g i) -> p g i", g=64)
    tmpbb2 = SC.r1(12288, 4096).rearrange("p (g i) -> p g i", g=64)
    t_t2 = SC.rt(12288, 4096)
    V(lambda e: e.tensor_tensor(out=tmpbb2[:], in0=bA[:], in1=bc(3), op=ALU.mult), [t_b, t_pl], [t_t2])
    V(lambda e: e.tensor_tensor(out=tmpbb[:], in0=bB[:], in1=bc(4), op=ALU.mult), [t_b, t_pl], [t_b])
    V(lambda e: e.tensor_tensor(out=bbP[:], in0=tmpbb2[:], in1=tmpbb[:], op=ALU.add), [t_b, t_t2], [t_bb])
    V(lambda e: e.tensor_tensor(out=tmpbb2[:], in0=bB[:], in1=bc(5), op=ALU.mult), [t_b, t_pl], [t_t2])
    V(lambda e: e.tensor_tensor(out=tmpbb[:], in0=bA[:], in1=bc(6), op=ALU.mult), [t_b, t_pl], [t_b])
    V(lambda e: e.tensor_tensor(out=bbPB[:], in0=tmpbb2[:], in1=tmpbb[:], op=ALU.add), [t_b, t_t2], [t_bb])
    c_re = w["s5_c_re"][0].rearrange("g j p -> (g j) p")
    c_im = w["s5_c_im"][0].rearrange("g j p -> (g j) p")
    for r in range(8):
        sgt = stage[r % 2]
        S.dma("sp", sgt[:, 0:64], c_re[r * 128:(r + 1) * 128, :], writes=[t_stage[r % 2]])
        S.dma("sp", sgt[:, 64:128], c_im[r * 128:(r + 1) * 128, :], writes=[t_stage[r % 2]])
        S.dma("sp", sgt[:, 128:192], c_im[r * 128:(r + 1) * 128, :], writes=[t_stage[r % 2]])
        S.dma("sp", sgt[:, 192:256], c_re[r * 128:(r + 1) * 128, :], writes=[t_stage[r % 2]])
        dA = cA[:, 8 * r:8 * r + 8, :].rearrange("p g j -> p (g j)")
        dB = cB[:, 8 * r:8 * r + 8, :].rearrange("p g j -> p (g j)")
        transpose_to(lambda p, dA=dA: (lambda e: e.tensor_scalar(out=dA, in0=p, scalar1=sgn[:, 0:1], scalar2=None, op0=ALU.mult)),
                     sgt[:, 0:128], 128, 128, 6, [t_stage[r % 2]], [t_c], eng="dve")
        transpose_to(lambda p, dB=dB: (lambda e: e.tensor_scalar(out=dB, in0=p, scalar1=-1.0, scalar2=None, op0=ALU.mult)),
                     sgt[:, 128:256], 128, 128, 7, [t_stage[r % 2]], [t_c], eng="dve")

    def rmsnorm(C, dst, t_dst, gi, N, f32out=False):
        for kc in range(8):
            A(lambda e, kc=kc: e.activation(out=C.sq[:, kc, 0:N], in_=C.xres[:, kc, 0:N], func=AF.Square), [C.t_xres], [C.t_sq])
        for kc in range(8):
            PE(lambda e, kc=kc: e.matmul(ps[4][:, 0:N], lhsT=ones_b[:], rhs=C.sq[:, kc, 0:N], start=(kc == 0), stop=(kc == 7)),
               [C.t_sq, t_const], [t_ps[4]])
        A(lambda e: e.activation(out=tmpf[0][:, 0:N], in_=ps[4][:, 0:N], func=AF.Sqrt, scale=1.0 / D, bias=epsc), [t_ps[4], t_const], [t_tmpf[0]])
        V(lambda e: e.reciprocal(out=tmpf[0][:, 0:N], in_=tmpf[0][:, 0:N]), [t_tmpf[0]], [t_tmpf[0]])
        for kc in range(8):
            V(lambda e, kc=kc: e.scalar_tensor_tensor(out=dst[:, kc, 0:N], in0=C.xres[:, kc, 0:N], scalar=gains[:, gi, kc:kc + 1],
                                                    in1=tmpf[0][:, 0:N], op0=ALU.mult, op1=ALU.mult),
              [C.t_xres, t_tmpf[0], t_const], [t_dst])

    def cost(x):
        return float(x)

    class Lag:
        def __init__(self):
            self.p = None

        def push(self, fn):
            if self.p is not None:
                self.p()
            self.p = fn

        def flush(self):
            if self.p is not None:
                self.p()
            self.p = None

    bank_rr = {"i": 0}

    def next_bank():
        b = bank_rr["i"]
        bank_rr["i"] = (b + 1) % 4
        return b

    def dense_chunk(src, t_src, KC, wslot, t_wslot, mi, N, bank):
        for kc in range(KC):
            PE(lambda e, kc=kc: e.matmul(ps[bank][:, 0:N], lhsT=wslot[:, kc, mi * 128:(mi + 1) * 128], rhs=src[:, kc, 0:N],
                                         start=(kc == 0), stop=(kc == KC - 1)), [t_src, t_wslot], [t_ps[bank]])

    def dense_resid(C, src, t_src, N, wname, l):
        lag = Lag()
        for blk in range(2):
            ws, tw = wget((wname, l, 0, 1024, blk * 512, 512))
            for mi in range(4):
                b = next_bank()
                dense_chunk(src, t_src, 8, ws, tw, mi, N, b)
                m = blk * 4 + mi
                lag.push(lambda m=m, b=b: V(lambda e: e.tensor_tensor(out=C.xres[:, m, 0:N], in0=ps[b][:, 0:N], in1=C.xres[:, m, 0:N], op=ALU.add),
                                            [t_ps[b], C.t_xres], [C.t_xres]))
            lag.flush()
            yield cost(8)

    def glu_dense(src, t_src, N, bias, consume, wname):
        lag = Lag()
        for half in range(2):
            wa, twa = wget((wname, 0, 0, 1024, half * 512, 512))
            wg, twg = wget((wname, 0, 0, 1024, 1024 + half * 512, 512))
            for mi in range(4):
                m = half * 4 + mi
                ba, bg = next_bank(), next_bank()
                dense_chunk(src, t_src, 8, wa, twa, mi, N, ba)
                dense_chunk(src, t_src, 8, wg, twg, mi, N, bg)
                ti = 1 + (m % 2)

                def ev(m=m, ba=ba, bg=bg, ti=ti):
                    if bias:
                        A(lambda e: e.activation(out=tmpf[ti][:, 0:N], in_=ps[bg][:, 0:N], func=AF.Sigmoid, bias=cvec[:, 0, 8 + m:9 + m]),
                          [t_ps[bg], t_const], [t_tmpf[ti]])
                    else:
                        A(lambda e: e.activation(out=tmpf[ti][:, 0:N], in_=ps[bg][:, 0:N], func=AF.Sigmoid), [t_ps[bg]], [t_tmpf[ti]])
                    consume(m, ba, ti)
                lag.push(ev)
            lag.flush()
            yield cost(9)

    for r in range(2):
        S.dma("sp", stage[r][:], mem[r * 128:(r + 1) * 128, :], writes=[t_stage[r]])
        for kc in range(8):
            transpose_to(lambda p, kc=kc, r=r: (lambda e: e.activation(out=memT[:, kc, r * 128:(r + 1) * 128], in_=p, func=AF.Copy)),
                         stage[r][:, kc * 128:(kc + 1) * 128], 128, 128, 6 + (kc % 2), [t_stage[r]], [t_memT])
    for l in range(2):
        for which in range(2):
            for nb in range(2):
                ws, tw = wget(("xattn_w_k" if which == 0 else "xattn_w_v", l, 0, 1024, nb * 512, 512))
                for mt in range(2):
                    b = next_bank()
                    for kc in range(8):
                        PE(lambda e, kc=kc, mt=mt, b=b, ws=ws: e.matmul(ps[b][:], lhsT=memT[:, kc, mt * 128:(mt + 1) * 128], rhs=ws[:, kc, :],
                                                                    start=(kc == 0), stop=(kc == 7)), [t_memT, tw], [t_ps[b]])
                    si = mt
                    A(lambda e, b=b, si=si: e.activation(out=stage[si][:, 0:512], in_=ps[b][:], func=AF.Copy), [t_ps[b]], [t_stage[si]])
                    dst = (mkp if which == 0 else mvp)[l, mt * 128:(mt + 1) * 128, nb * 512:(nb + 1) * 512]
                    out_toks.append(S.dma("sp", dst, stage[si][:, 0:512], reads=[t_stage[si]]))
                    if which == 1:
                        V(lambda e, l=l, mt=mt, nb=nb, si=si: e.tensor_copy(out=Vb[l][:, mt, nb * 512:(nb + 1) * 512], in_=stage[si][:, 0:512]),
                          [t_stage[si]], [t_Vb[l]])
                if which == 0:
                    for fi in range(4):
                        b = next_bank()
                        for kc in range(8):
                            PE(lambda e, kc=kc, fi=fi, b=b, ws=ws: e.matmul(ps[b][:, 0:256], lhsT=ws[:, kc, fi * 128:(fi + 1) * 128], rhs=memT[:, kc, :],
                                                                        start=(kc == 0), stop=(kc == 7)), [t_memT, tw], [t_ps[b]])
                        V(lambda e, l=l, fc=nb * 4 + fi, b=b: e.tensor_copy(out=KT[l][:, fc, :], in_=ps[b][:, 0:256]), [t_ps[b]], [t_KT[l]])

    def run_tile(ti, C):
        sample = (ti == 4)
        N = 64 if sample else NT
        t0 = ti * NT
        if not sample:
            for r in range(4):
                sgt, tsg = stage[r % 2], t_stage[r % 2]
                S.dma("sp", sgt[:], xp[t0 + r * 128:t0 + (r + 1) * 128, :], writes=[tsg])
                for kc in range(8):
                    transpose_to(lambda p, kc=kc, r=r: (lambda e: e.activation(out=C.xres[:, kc, r * 128:(r + 1) * 128], in_=p, func=AF.Copy)),
                                 sgt[:, kc * 128:(kc + 1) * 128], 128, 128, 6 + (kc % 2), [tsg], [C.t_xres])
                yield cost(4)
        else:
            S.dma("sp", stage[0][0:64, :], xs, writes=[t_stage[0]])
            for kc in range(8):
                transpose_to(lambda p, kc=kc: (lambda e: e.activation(out=C.xres[:, kc, 0:64], in_=p, func=AF.Copy)),
                             stage[0][0:64, kc * 128:(kc + 1) * 128], 64, 128, 6 + (kc % 2), [t_stage[0]], [C.t_xres])
            yield cost(3)

        def dump():
            if DEBUG and ti == 0:
                out_toks.append(S.dma("sp", dbg[dbg_n[0]], C.xres[:], reads=[C.t_xres]))
                dbg_n[0] += 1
        dump()
        for l in range(2):
            rmsnorm(C, C.xn, C.t_xn, l, N)
            yield cost(6)
            if l == 0:
                yield "need:conv"
                yield from conv_mixer(C, ti, sample, N)
            else:
                yield "need:s5"
                yield "done:in_s5"
                yield from s5_mixer(C, ti, sample, N)
                yield "done:s5"
            dump()
            rmsnorm(C, C.xn, C.t_xn, 2 + l, N)
            qlag = Lag()
            for blk in range(2):
                ws, tw = wget(("xattn_w_q", l, 0, 1024, blk * 512, 512))
                for mi in range(4):
                    b = next_bank()
                    dense_chunk(C.xn, C.t_xn, 8, ws, tw, mi, N, b)
                    qlag.push(lambda m=blk * 4 + mi, b=b: A(lambda e: e.activation(out=C.actb[:, m, 0:N], in_=ps[b][:, 0:N], func=AF.Identity, scale=1.0 / 16.0),
                                                           [t_ps[b]], [C.t_actb]))
                qlag.flush()
                yield cost(8)
            if not sample:
                yield from attn_prompt(C, l, N)
            else:
                yield from attn_sample(C, l)
            yield from dense_resid(C, C.xn, C.t_xn, N, "xattn_w_o", l)
            dump()
            rmsnorm(C, C.xn, C.t_xn, 4 + l, N)
            yield cost(6)
            mlag = Lag()
            for half in range(2):
                for blk in range(4):
                    ws, tw = wget(("mlp_w_up", l, 0, 1024, half * 2048 + blk * 512, 512))
                    for mi in range(4):
                        b = next_bank()
                        dense_chunk(C.xn, C.t_xn, 8, ws, tw, mi, N, b)
                        m = blk * 4 + mi
                        tb = m % 4
                        def evu(m=m, b=b, tb=tb):
                            A(lambda e: e.activation(out=tmpb[tb][:, 0:N], in_=ps[b][:, 0:N], func=AF.Relu), [t_ps[b]], [t_tmpb[tb]])
                            V(lambda e: e.tensor_tensor(out=C.hid[:, m, 0:N], in0=tmpb[tb][:, 0:N], in1=tmpb[tb][:, 0:N], op=ALU.mult),
                              [t_tmpb[tb]], [C.t_hid])
                        mlag.push(evu)
                    mlag.flush()
                    yield cost(8)
                for m in range(8):
                    ws, tw = wget(("mlp_w_down", l, half * 2048, 2048, m * 128, 128))
                    b = next_bank()
                    dense_chunk(C.hid, C.t_hid, 16, ws, tw, 0, N, b)
                    mlag.push(lambda m=m, b=b: V(lambda e: e.tensor_tensor(out=C.xres[:, m, 0:N], in0=ps[b][:, 0:N], in1=C.xres[:, m, 0:N], op=ALU.add),
                                                 [t_ps[b], C.t_xres], [C.t_xres]))
                    if m % 2 == 1:
                        mlag.flush()
                        yield cost(8)
        dump()
        rmsnorm(C, C.cf, C.t_cf, 6, N, f32out=True)
        yield cost(6)
        if not sample:
            for r in range(4):
                sgt, tsg = stage[r % 2], t_stage[r % 2]
                for kc in range(8):
                    bk = 6 + (kc % 2)
                    PE(lambda e, kc=kc, r=r, bk=bk: e.transpose(out=ps[bk][:, 0:128], in_=C.cf[:, kc, r * 128:(r + 1) * 128], identity=ident_f[:]),
                       [C.t_cf, t_const], [t_ps[bk]])
                    A(lambda e, kc=kc, bk=bk, sgt=sgt: e.activation(out=sgt[:, kc * 128:(kc + 1) * 128], in_=ps[bk][:, 0:128], func=AF.Copy), [t_ps[bk]], [tsg])
                out_toks.append(S.dma("sp", yp[t0 + r * 128:t0 + (r + 1) * 128, :], sgt[:], reads=[tsg]))
                yield cost(4)
        else:
            for kc in range(8):
                bk = 6 + (kc % 2)
                PE(lambda e, kc=kc, bk=bk: e.transpose(out=ps[bk][0:64, 0:128], in_=C.cf[:, kc, 0:64], identity=ident_f[:]),
                   [C.t_cf, t_const], [t_ps[bk]])
                A(lambda e, kc=kc, bk=bk: e.activation(out=stage[0][0:64, kc * 128:(kc + 1) * 128], in_=ps[bk][0:64, 0:128], func=AF.Copy), [t_ps[bk]], [t_stage[0]])
            out_toks.append(S.dma("sp", ys, stage[0][0:64, :], reads=[t_stage[0]]))
            yield cost(3)

    def conv_mixer(C, ti, sample, N):
        if sample:
            for r in range(4):
                sgt, tsg = stage[r % 2], t_stage[r % 2]
                S.dma("sp", sgt[0:120, :], cc[4 * r:4 * r + 4].rearrange("b j c -> (b j) c"), writes=[tsg])
                for kc in range(8):
                    transpose_to(lambda p, kc=kc, r=r: (lambda e: e.activation(
                        out=C.gpads[:, kc, 4 * r:4 * r + 4, 0:30], in_=p.rearrange("p (b j) -> p b j", b=4), func=AF.Copy)),
                        sgt[0:120, kc * 128:(kc + 1) * 128], 120, 128, 6 + (kc % 2), [tsg], [C.t_gpads])
                out_toks.append(S.dma("sp", convs[4 * r:4 * r + 4, 0:26, :], cc[4 * r:4 * r + 4, 4:30, :]))
        last = (ti == 3)
        if not sample:
            V(lambda e: e.tensor_copy(out=C.gpad[:, :, 0:30], in_=hist[:]), [t_hist], [C.t_gpad])

        def consume(m, ba, tix):
            if sample:
                dstb = C.gpads[:, m, :, 30:34]
                srcp = ps[ba][:, 0:64].rearrange("p (b l) -> p b l", b=16)
                sg = tmpf[tix][:, 0:64].rearrange("p (b l) -> p b l", b=16)
                V(lambda e: e.scalar_tensor_tensor(out=dstb, in0=srcp, scalar=cvec[:, 0, m:m + 1], in1=sg, op0=ALU.add, op1=ALU.mult),
                  [t_ps[ba], t_tmpf[tix], t_const], [C.t_gpads])
                V(lambda e: e.scalar_tensor_tensor(out=gl30[:, m, 0:64], in0=ps[ba][:, 0:64], scalar=cvec[:, 0, m:m + 1], in1=tmpf[tix][:, 0:64],
                                                   op0=ALU.add, op1=ALU.mult), [t_ps[ba], t_tmpf[tix], t_const], [t_gl30])
            else:
                V(lambda e: e.scalar_tensor_tensor(out=C.gpad[:, m, 30:30 + N], in0=ps[ba][:, 0:N], scalar=cvec[:, 0, m:m + 1], in1=tmpf[tix][:, 0:N],
                                                   op0=ALU.add, op1=ALU.mult), [t_ps[ba], t_tmpf[tix], t_const], [C.t_gpad])
                if last:
                    V(lambda e: e.scalar_tensor_tensor(out=gl30[:, m, 0:30], in0=ps[ba][:, N - 30:N], scalar=cvec[:, 0, m:m + 1],
                                                       in1=tmpf[tix][:, N - 30:N], op0=ALU.add, op1=ALU.mult),
                      [t_ps[ba], t_tmpf[tix], t_const], [t_gl30])

        yield from glu_dense(C.xn, C.t_xn, N, True, consume, "conv_w_in")
        if sample:
            for kc in range(8):
                transpose_to(lambda p, kc=kc: (lambda e: e.activation(out=stage[1][0:64, kc * 128:(kc + 1) * 128], in_=p, func=AF.Copy)),
                             gl30[:, kc, 0:64], 128, 64, 6 + (kc % 2), [t_gl30], [t_stage[1]])
            for b_ in range(16):
                out_toks.append(S.dma("sp", convs[b_, 26:30, :], stage[1][4 * b_:4 * b_ + 4, :], reads=[t_stage[1]]))
        elif last:
            for kc in range(8):
                transpose_to(lambda p, kc=kc: (lambda e: e.activation(out=stage[1][0:30, kc * 128:(kc + 1) * 128], in_=p, func=AF.Copy)),
                             gl30[:, kc, 0:30], 128, 30, 6 + (kc % 2), [t_gl30], [t_stage[1]])
            out_toks.append(S.dma("sp", convp, stage[1][0:30, :], reads=[t_stage[1]]))
        dcnt = 0
        for kc in range(8):
            b = next_bank()
            for k in range(31):
                di_ = dcnt % 4
                dcnt += 1
                if k % 2 == 0:
                    V(lambda e, kc=kc, k=k, di_=di_: e.tensor_scalar(out=diag[di_][:], in0=ident_b[:], scalar1=dwt[:, kc, k:k + 1], scalar2=None, op0=ALU.mult),
                      [t_const], [t_diag[di_]])
                else:
                    A(lambda e, kc=kc, k=k, di_=di_: e.activation(out=diag[di_][:], in_=ident_b[:], func=AF.Identity, scale=dwt[:, kc, k:k + 1]),
                      [t_const], [t_diag[di_]])
                if sample:
                    PE(lambda e, kc=kc, k=k, di_=di_, b=b: e.matmul(ps[b][:, 0:64].rearrange("p (b l) -> p b l", b=16), lhsT=diag[di_][:],
                                                                    rhs=C.gpads[:, kc, :, k:k + 4], start=(k == 0), stop=(k == 30)),
                       [t_diag[di_], C.t_gpads], [t_ps[b]])
                else:
                    PE(lambda e, kc=kc, k=k, di_=di_, b=b: e.matmul(ps[b][:, 0:N], lhsT=diag[di_][:], rhs=C.gpad[:, kc, k:k + N],
                                                                    start=(k == 0), stop=(k == 30)), [t_diag[di_], C.t_gpad], [t_ps[b]])
            A(lambda e, kc=kc, b=b: e.activation(out=C.cf[:, kc, 0:N], in_=ps[b][:, 0:N], func=AF.Identity, bias=cvec[:, 1, kc:kc + 1]),
              [t_ps[b], t_const], [C.t_cf])
            if kc % 2 == 1:
                yield cost(15)
        if not sample:
            V(lambda e: e.tensor_copy(out=hist[:], in_=C.gpad[:, :, N:N + 30]), [C.t_gpad], [t_hist])
        yield "done:conv"
        for kc in range(8):
            i0, i1 = 2 * (kc % 2), 2 * (kc % 2) + 1
            V(lambda e, kc=kc, i0=i0: e.tensor_copy(out=tmpb[i0][:, 0:N], in_=C.cf[:, kc, 0:N]), [C.t_cf], [t_tmpb[i0]])
            A(lambda e, kc=kc, i1=i1: e.activation(out=tmpb[i1][:, 0:N], in_=C.cf[:, kc, 0:N], func=AF.Square), [C.t_cf], [t_tmpb[i1]])
            PE(lambda e, kc=kc, i0=i0: e.matmul(ps[4][:, 0:N], lhsT=ones_b[:], rhs=tmpb[i0][:, 0:N], start=(kc == 0), stop=(kc == 7)),
               [t_tmpb[i0], t_const], [t_ps[4]])
            PE(lambda e, kc=kc, i1=i1: e.matmul(ps[3][:, 0:N], lhsT=ones_b[:], rhs=tmpb[i1][:, 0:N], start=(kc == 0), stop=(kc == 7)),
               [t_tmpb[i1], t_const], [t_ps[3]])
        mu, ms, rs = tmpf[1], tmpf[2], tmpf[3]
        A(lambda e: e.activation(out=mu[:, 0:N], in_=ps[4][:, 0:N], func=AF.Identity, scale=1.0 / D), [t_ps[4]], [t_tmpf[1]])
        V(lambda e: e.tensor_tensor(out=ms[:, 0:N], in0=mu[:, 0:N], in1=mu[:, 0:N], op=ALU.mult), [t_tmpf[1]], [t_tmpf[2]])
        V(lambda e: e.scalar_tensor_tensor(out=ms[:, 0:N], in0=ps[3][:, 0:N], scalar=1.0 / D, in1=ms[:, 0:N], op0=ALU.mult, op1=ALU.subtract),
          [t_ps[3], t_tmpf[2]], [t_tmpf[2]])
        A(lambda e: e.activation(out=rs[:, 0:N], in_=ms[:, 0:N], func=AF.Sqrt, bias=epsc), [t_tmpf[2], t_const], [t_tmpf[3]])
        V(lambda e: e.reciprocal(out=rs[:, 0:N], in_=rs[:, 0:N]), [t_tmpf[3]], [t_tmpf[3]])
        for kc in range(8):
            V(lambda e, kc=kc: e.tensor_tensor(out=C.cf[:, kc, 0:N], in0=C.cf[:, kc, 0:N], in1=mu[:, 0:N], op=ALU.subtract), [C.t_cf, t_tmpf[1]], [C.t_cf])
            V(lambda e, kc=kc: e.tensor_tensor(out=C.cf[:, kc, 0:N], in0=C.cf[:, kc, 0:N], in1=rs[:, 0:N], op=ALU.mult), [C.t_cf, t_tmpf[3]], [C.t_cf])
            A(lambda e, kc=kc: e.activation(out=C.xn[:, kc, 0:N], in_=C.cf[:, kc, 0:N], func=AF.Silu, scale=cvec[:, 2, kc:kc + 1], bias=cvec[:, 3, kc:kc + 1]),
              [C.t_cf, t_const], [C.t_xn])
        yield cost(8)
        yield from dense_resid(C, C.xn, C.t_xn, N, "conv_w_out", 0)

    def attn_prompt(C, l, N):
        for h in range(4):
            for mc in range(2):
                for ec in range(2):
                    PE(lambda e, mc=mc, ec=ec, h=h: e.matmul(ps[mc][:, 0:N], lhsT=KT[l][:, 2 * h + ec, mc * 128:(mc + 1) * 128], rhs=C.actb[:, 2 * h + ec, 0:N],
                                                     start=(ec == 0), stop=(ec == 1)), [t_KT[l], C.t_actb], [t_ps[mc]])
                A(lambda e, mc=mc: e.activation(out=Eb[mc][:, 0:N], in_=ps[mc][:, 0:N], func=AF.Exp), [t_ps[mc]], [t_Eb[mc]])
            for mc in range(2):
                PE(lambda e, mc=mc: e.matmul(ps[2][:, 0:N], lhsT=ones_b[:], rhs=Eb[mc][:, 0:N], start=(mc == 0), stop=(mc == 1)),
                   [t_Eb[mc], t_const], [t_ps[2]])
            V(lambda e: e.reciprocal(out=tmpf[1][:, 0:N], in_=ps[2][:, 0:N]), [t_ps[2]], [t_tmpf[1]])
            for ec in range(2):
                bk = 3 + ec
                for mc in range(2):
                    PE(lambda e, mc=mc, ec=ec, bk=bk, h=h: e.matmul(ps[bk][:, 0:N], lhsT=Vb[l][:, mc, (2 * h + ec) * 128:(2 * h + ec + 1) * 128], rhs=Eb[mc][:, 0:N],
                                                             start=(mc == 0), stop=(mc == 1)), [t_Vb[l], t_Eb[mc]], [t_ps[bk]])
                V(lambda e, ec=ec, bk=bk, h=h: e.tensor_tensor(out=C.xn[:, 2 * h + ec, 0:N], in0=ps[bk][:, 0:N], in1=tmpf[1][:, 0:N], op=ALU.mult),
                  [t_ps[bk], t_tmpf[1]], [C.t_xn])
            yield cost(6)

    def attn_sample(C, l):
        sc_v = ps[0][:].rearrange("p (b h m l) -> p b h m l", b=16, h=4, m=2)
        first = True
        for b_ in range(16):
            i = 0
            S.dma("pool", C.Kc[i][:], cmk[l, b_].rearrange("(mt p) c -> p mt c", p=128), writes=[C.t_Kc[i]])
            for mt in range(2):
                for fq in range(2):
                    bk = 6 + fq
                    pv = ps[bk][:].bitcast(BF16)
                    for f4 in range(4):
                        fc = fq * 4 + f4
                        PE(lambda e, mt=mt, fc=fc, f4=f4, pv=pv, i=i: e.transpose(out=pv[:, f4 * 128:(f4 + 1) * 128], in_=C.Kc[i][:, mt, fc * 128:(fc + 1) * 128],
                                                                              identity=ident_b[:]), [C.t_Kc[i], t_const], [t_ps[bk]])
                    V(lambda e, mt=mt, fq=fq, pv=pv, i=i: e.tensor_copy(out=C.KTs[i][:, fq * 4:fq * 4 + 4, mt * 128:(mt + 1) * 128],
                                                                       in_=pv[:, 0:512].rearrange("p (f m) -> p f m", f=4)), [t_ps[bk]], [C.t_KTs[i]])
            for h in range(4):
                for mc in range(2):
                    for ec in range(2):
                        fst = first
                        first = False
                        PE(lambda e, h=h, mc=mc, ec=ec, fst=fst, i=i, b_=b_: e.matmul(sc_v[:, b_, h, mc, :], lhsT=C.KTs[i][:, 2 * h + ec, mc * 128:(mc + 1) * 128],
                                                                                 rhs=C.actb[:, 2 * h + ec, 4 * b_:4 * b_ + 4], start=fst, stop=True, skip_group_check=True),
                           [C.t_KTs[i], C.t_actb], [t_ps[0]])
        A(lambda e: e.activation(out=Eb[0][:], in_=ps[0][:], func=AF.Exp), [t_ps[0]], [t_Eb[0]])
        Ev = Eb[0][:].rearrange("p (b h m l) -> p b h m l", b=16, h=4, m=2)
        Ev2 = Eb[0][:].rearrange("p (bh m l) -> p bh m l", bh=64, m=2)
        for mc in range(2):
            PE(lambda e, mc=mc: e.matmul(ps[1][:, 0:256].rearrange("p (bh l) -> p bh l", bh=64), lhsT=ones_b[:], rhs=Ev2[:, :, mc, :],
                                         start=(mc == 0), stop=(mc == 1)), [t_Eb[0], t_const], [t_ps[1]])
        V(lambda e: e.reciprocal(out=tmpf[1][:, 0:256], in_=ps[1][:, 0:256]), [t_ps[1]], [t_tmpf[1]])
        ov = ps[2][:].rearrange("p (f b l) -> p f b l", f=8, b=16)
        first = True
        for b_ in range(16):
            i = 0
            S.dma("pool", C.Vc[i][:], cmv[l, b_].rearrange("(mt p) c -> p mt c", p=128), writes=[C.t_Vc[i]])
            for fc in range(8):
                for mc in range(2):
                    fst = first
                    first = False
                    PE(lambda e, fc=fc, mc=mc, fst=fst, i=i, b_=b_: e.matmul(ov[:, fc, b_, :], lhsT=C.Vc[i][:, mc, fc * 128:(fc + 1) * 128], rhs=Ev[:, b_, fc // 2, mc, :],
                                                                        start=fst, stop=True, skip_group_check=True), [C.t_Vc[i], t_Eb[0]], [t_ps[2]])
        rv = tmpf[1][:, 0:256].rearrange("p (b h l) -> p b h l", b=16, h=4)
        for fc in range(8):
            V(lambda e, fc=fc: e.tensor_tensor(out=C.xn[:, fc, 0:64].rearrange("p (b l) -> p b l", b=16), in0=ov[:, fc, :, :], in1=rv[:, :, fc // 2, :], op=ALU.mult),
              [t_ps[2], t_tmpf[1]], [C.t_xn])
        yield cost(60)

    def s5_prompt_loop(C, jit, tables, xaxb, data, ymm, N):
        tables(0)
        for kc in range(8):
            jit(kc)
            for k in range(8):
                g = 8 * kc + k
                xaxb(g)
                if g + 1 < 64:
                    tables(g + 1)
                data(g)
                if k > 0:
                    ymm(g - 1)
                yield cost(7)
            ymm(8 * kc + 7)
            tf = tmpf[1 + (kc % 2)]
            ttf = t_tmpf[1 + (kc % 2)]
            V(lambda e, kc=kc, tf=tf: e.scalar_tensor_tensor(out=tf[:, 0:N], in0=C.xn[:, kc, 0:N], scalar=cvec[:, 4, kc:kc + 1], in1=ps[5][:, 0:N],
                                                           op0=ALU.mult, op1=ALU.add), [C.t_xn, t_ps[5], t_const], [ttf])
            A(lambda e, kc=kc, tf=tf: e.activation(out=C.actb[:, kc, 0:N], in_=tf[:, 0:N], func=AF.Gelu_apprx_tanh), [ttf], [C.t_actb])
            yield cost(3)

    def s5_sample_batched(C, jit):
        r1, rt = C.r1, C.rt
        SINs = r1(0, 1024).rearrange("p (g l) -> p g l", g=64); tS = rt(0, 1024)
        COSs = r1(1024, 1024).rearrange("p (g l) -> p g l", g=64); tCo = rt(1024, 1024)
        UA = r1(2048, 1024).rearrange("p (g l) -> p g l", g=64); tUA = rt(2048, 1024)
        KK = r1(3072, 1024).rearrange("p (g l) -> p g l", g=64); tKK = rt(3072, 1024)
        F = r1(20480, 2048).rearrange("p (k b l) -> p k b l", k=8, b=16); tF = rt(20480, 2048)
        Gq = r1(22528, 2048).rearrange("p (k b l) -> p k b l", k=8, b=16); tG = rt(22528, 2048)
        QC = r1(18432, 1024, BF16); QS = r1(19456, 1024, BF16); tQ = rt(18432, 2048)
        TMP = r1(3072, 512).rearrange("p (k b) -> p k b", k=8)
        th = pl[:, 1, :].unsqueeze(2).to_broadcast([128, 64, 4])
        i4 = iotas[:, 0:4].unsqueeze(1).to_broadcast([128, 64, 4])
        V(lambda e: e.tensor_tensor(out=UA, in0=th, in1=i4, op=ALU.mult), [t_pl, t_const], [tUA])
        V(lambda e: e.tensor_scalar(out=KK, in0=UA, scalar1=MAGIC, scalar2=-MAGIC, op0=ALU.add, op1=ALU.add), [tUA], [tKK])
        V(lambda e: e.tensor_tensor(out=KK, in0=KK, in1=UA, op=ALU.subtract), [tUA, tKK], [tKK])
        A(lambda e: e.activation(out=SINs, in_=KK, func=AF.Sin, scale=-SC2PI), [tKK], [tS])
        A(lambda e: e.activation(out=UA, in_=KK, func=AF.Abs), [tKK], [tUA])
        A(lambda e: e.activation(out=COSs, in_=UA, func=AF.Sin, scale=-SC2PI, bias=hpi), [tUA, t_const], [tCo])
        yield cost(4)
        for kc in range(8):
            jit(kc)
            gs = slice(8 * kc, 8 * kc + 8)
            for k in range(8):
                PE(lambda e, k=k, kc=kc: e.matmul(ps[2][:, k * 64:(k + 1) * 64], lhsT=BpA[:, k, :], rhs=C.xn[:, kc, 0:64], start=True, stop=True, skip_group_check=True),
                   [t_Bp, C.t_xn], [t_ps[2]])
                PE(lambda e, k=k, kc=kc: e.matmul(ps[3][:, k * 64:(k + 1) * 64], lhsT=BpB[:, k, :], rhs=C.xn[:, kc, 0:64], start=True, stop=True, skip_group_check=True),
                   [t_Bp, C.t_xn], [t_ps[3]])
            cb = COSs[:, gs, :].unsqueeze(2).to_broadcast([128, 8, 16, 4])
            sb_ = SINs[:, gs, :].unsqueeze(2).to_broadcast([128, 8, 16, 4])
            xa = ps[2][:].rearrange("p (k b l) -> p k b l", k=8, b=16)
            xb = ps[3][:].rearrange("p (k b l) -> p k b l", k=8, b=16)
            V(lambda e, xa=xa, cb=cb: e.tensor_tensor(out=F, in0=xa, in1=cb, op=ALU.mult), [t_ps[2], tCo], [tF])
            V(lambda e, xb=xb, sb_=sb_: e.tensor_tensor(out=Gq, in0=xb, in1=sb_, op=ALU.mult), [t_ps[3], tS], [tG])
            V(lambda e: e.tensor_tensor(out=F, in0=F, in1=Gq, op=ALU.add), [tF, tG], [tF])
            rb = pl[:, 2, gs].unsqueeze(2).to_broadcast([128, 8, 16])
            for l_ in range(4):
                prev = C.H0[:, gs, :] if l_ == 0 else Gq[:, :, :, l_ - 1]
                V(lambda e, prev=prev, rb=rb: e.tensor_tensor(out=TMP, in0=prev, in1=rb, op=ALU.mult), [C.t_H0, tG, t_pl], [tKK])
                V(lambda e, l_=l_: e.tensor_tensor(out=Gq[:, :, :, l_], in0=TMP, in1=F[:, :, :, l_], op=ALU.add), [tKK, tF], [tG])
            V(lambda e, cb=cb: e.tensor_tensor(out=QC.rearrange("p (k b l) -> p k b l", k=8, b=16), in0=Gq, in1=cb, op=ALU.mult), [tG, tCo], [tQ])
            V(lambda e, sb_=sb_: e.tensor_tensor(out=QS.rearrange("p (k b l) -> p k b l", k=8, b=16), in0=Gq, in1=sb_, op=ALU.mult), [tG, tS], [tQ])
            c3 = COSs[:, gs, 3:4].to_broadcast([128, 8, 16])
            s3 = SINs[:, gs, 3:4].to_broadcast([128, 8, 16])
            V(lambda e, gs=gs, c3=c3: e.tensor_tensor(out=C.QCLs[:, gs, :], in0=Gq[:, :, :, 3], in1=c3, op=ALU.mult), [tG, tCo], [C.t_QLs])
            V(lambda e, gs=gs, s3=s3: e.tensor_tensor(out=C.QSLs[:, gs, :], in0=Gq[:, :, :, 3], in1=s3, op=ALU.mult), [tG, tS], [C.t_QLs])
            for k in range(8):
                PE(lambda e, k=k: e.matmul(ps[5][:, 0:64], lhsT=CpA[:, k, :], rhs=QC[:, k * 64:(k + 1) * 64], start=(k == 0), stop=False), [t_Cp, tQ], [t_ps[5]])
                PE(lambda e, k=k: e.matmul(ps[5][:, 0:64], lhsT=CpB[:, k, :], rhs=QS[:, k * 64:(k + 1) * 64], start=False, stop=(k == 7)), [t_Cp, tQ], [t_ps[5]])
            tf = tmpf[1 + (kc % 2)]
            ttf = t_tmpf[1 + (kc % 2)]
            V(lambda e, kc=kc, tf=tf: e.scalar_tensor_tensor(out=tf[:, 0:64], in0=C.xn[:, kc, 0:64], scalar=cvec[:, 4, kc:kc + 1], in1=ps[5][:, 0:64],
                                                           op0=ALU.mult, op1=ALU.add), [C.t_xn, t_ps[5], t_const], [ttf])
            A(lambda e, kc=kc, tf=tf: e.activation(out=C.actb[:, kc, 0:64], in_=tf[:, 0:64], func=AF.Gelu_apprx_tanh), [ttf], [C.t_actb])
            yield cost(10)

    def s5_mixer(C, ti, sample, N):
        last = (ti == 3)
        io = iotas[:, 0:64] if sample else iota[:, 0:NT]
        t0f = 0.0 if sample else float(ti * NT)
        V(lambda e: e.tensor_scalar(out=C.bU[:], in0=pl[:, 1, :], scalar1=t0f, scalar2=None, op0=ALU.mult), [t_pl], [C.t_bU])
        V(lambda e: e.tensor_scalar(out=tmpf[3][:, 0:64], in0=C.bU[:], scalar1=MAGIC, scalar2=-MAGIC, op0=ALU.add, op1=ALU.add), [C.t_bU], [t_tmpf[3]])
        V(lambda e: e.tensor_tensor(out=C.bU[:], in0=C.bU[:], in1=tmpf[3][:, 0:64], op=ALU.subtract), [C.t_bU, t_tmpf[3]], [C.t_bU])
        if sample:
            for r in range(8):
                S.dma("sp", C.HBr[:, :, 0:64], sre[:, r * 512:(r + 1) * 512].rearrange("b (g p) -> b g p", g=8), writes=[C.t_HBr])
                S.dma("sp", C.HBr[:, :, 64:128], sim[:, r * 512:(r + 1) * 512].rearrange("b (g p) -> b g p", g=8), writes=[C.t_HBr])
                for g_ in range(8):
                    PE(lambda e, g_=g_: e.transpose(out=ps[6][:, g_ * 16:(g_ + 1) * 16], in_=C.HBr[:, g_, :], identity=ident_f[0:16, 0:16]),
                       [C.t_HBr, t_const], [t_ps[6]])
                A(lambda e, r=r: e.activation(out=C.H0[:, 8 * r:8 * r + 8, :], in_=ps[6][:, 0:128].rearrange("p (g b) -> p g b", g=8), func=AF.Copy),
                  [t_ps[6]], [C.t_H0])
            yield cost(6)
        W = C.s5ws if sample else C.s5w
        TW = C.t_s5ws if sample else C.t_s5w
        ta_, tb_ = TW[0], TW[1]
        a_, b_ = W[0], W[1]

        def tables(g):
            si = g % 2
            d_, e_ = W[2 + 2 * si], W[3 + 2 * si]
            td_, te_ = TW[2 + 2 * si], TW[3 + 2 * si]
            A(lambda e: e.activation(out=a_[:, 0:N], in_=io, func=AF.Identity, scale=pl[:, 1, g:g + 1], bias=C.bU[:, g:g + 1]),
              [t_pl, t_const, C.t_bU], [ta_])
            A(lambda e: e.activation(out=b_[:, 0:N], in_=a_[:, 0:N], func=AF.Identity, bias=sgn_magic()), [ta_, t_const], [tb_])
            V(lambda e: e.scalar_tensor_tensor(out=b_[:, 0:N], in0=b_[:, 0:N], scalar=-MAGIC, in1=a_[:, 0:N], op0=ALU.add, op1=ALU.subtract),
              [ta_, tb_], [tb_])
            A(lambda e: e.activation(out=d_[:, 0:N], in_=b_[:, 0:N], func=AF.Sin, scale=-SC2PI), [tb_], [td_])
            A(lambda e: e.activation(out=a_[:, 0:N], in_=b_[:, 0:N], func=AF.Abs), [tb_], [ta_])
            A(lambda e: e.activation(out=e_[:, 0:N], in_=a_[:, 0:N], func=AF.Sin, scale=-SC2PI, bias=hpi), [ta_, t_const], [te_])

        def jit(kc):
            for (srcb, dstb, bank) in ((bbP, BpA, 6), (bbPB, BpB, 7)):
                V(lambda e, srcb=srcb: e.tensor_copy(out=_diag(PB[0]), in_=srcb[:, 8 * kc:8 * kc + 8, :]), [t_bb], [t_PB[0]])
                for k in range(8):
                    PE(lambda e, k=k, bank=bank: e.transpose(out=psqb(bank, k), in_=PB[0][:, k, :], identity=ident_b[:]), [t_PB[0], t_const], [t_ps[bank]])
                    if k % 4 == 3:
                        k0 = k - 3
                        A(lambda e, k0=k0, bank=bank, dstb=dstb: e.activation(out=dstb[:, k0:k0 + 4, :], in_=ps[bank][:].bitcast(BF16)[:, 0:512].rearrange("p (k c) -> p k c", k=4),
                                                                        func=AF.Copy), [t_ps[bank]], [t_Bp])
            V(lambda e: e.tensor_copy(out=_diag(CpA), in_=cA[:, 8 * kc:8 * kc + 8, :]), [t_c], [t_Cp])
            V(lambda e: e.tensor_copy(out=_diag(CpB), in_=cB[:, 8 * kc:8 * kc + 8, :]), [t_c], [t_Cp])

        def data(g):
            kc, k = g // 8, g % 8
            si = g % 2
            d_, e_, f_, q_ = W[2 + 2 * si], W[3 + 2 * si], W[6 + 2 * si], W[7 + 2 * si]
            td_, te_, tf_, tq_ = TW[2 + 2 * si], TW[3 + 2 * si], TW[6 + 2 * si], TW[7 + 2 * si]
            qc_, qs_, tqc_ = C.qcb[si], C.qsb[si], C.t_qcs[si]
            ba, bb_ = 2 * si, 2 * si + 1
            yb = 5
            V(lambda e: e.tensor_tensor(out=f_[:, 0:N], in0=ps[ba][:, 0:N], in1=e_[:, 0:N], op=ALU.mult), [t_ps[ba], te_], [tf_])
            V(lambda e: e.tensor_tensor(out=q_[:, 0:N], in0=ps[bb_][:, 0:N], in1=d_[:, 0:N], op=ALU.mult), [t_ps[bb_], td_], [tq_])
            V(lambda e: e.tensor_tensor(out=f_[:, 0:N], in0=f_[:, 0:N], in1=q_[:, 0:N], op=ALU.add), [tf_, tq_], [tf_])
            rcol = pl[:, 2, g:g + 1]
            if not sample:
                V(lambda e: e.tensor_tensor_scan(out=q_[:, 0:N], data0=rcol.to_broadcast([128, N]), data1=f_[:, 0:N],
                                                 initial=QST[:, g:g + 1], op0=ALU.mult, op1=ALU.add), [tf_, t_pl, t_QST], [tq_])
                A(lambda e: e.activation(out=QST[:, g:g + 1], in_=q_[:, N - 1:N], func=AF.Copy), [tq_], [t_QST])
            else:
                zv = f_[:, 0:64].rearrange("p (b l) -> p b l", b=16)
                qv = q_[:, 0:64].rearrange("p (b l) -> p b l", b=16)
                for l_ in range(4):
                    prev = C.H0[:, g, :] if l_ == 0 else qv[:, :, l_ - 1]
                    V(lambda e, l_=l_, prev=prev: e.scalar_tensor_tensor(out=qv[:, :, l_], in0=prev, scalar=rcol, in1=zv[:, :, l_],
                                                                      op0=ALU.mult, op1=ALU.add), [tf_, t_pl, C.t_H0, tq_], [tq_])
            V(lambda e: e.tensor_tensor(out=qc_[:, 0:N], in0=q_[:, 0:N], in1=e_[:, 0:N], op=ALU.mult), [tq_, te_], [tqc_])
            V(lambda e: e.tensor_tensor(out=qs_[:, 0:N], in0=q_[:, 0:N], in1=d_[:, 0:N], op=ALU.mult), [tq_, td_], [tqc_])
            if last:
                V(lambda e: e.tensor_tensor(out=QCL[:, g:g + 1], in0=q_[:, N - 1:N], in1=e_[:, N - 1:N], op=ALU.mult), [tq_, te_], [t_QL])
                V(lambda e: e.tensor_tensor(out=QSL[:, g:g + 1], in0=q_[:, N - 1:N], in1=d_[:, N - 1:N], op=ALU.mult), [tq_, td_], [t_QL])
            if sample:
                q3 = q_[:, 0:64].rearrange("p (b l) -> p b l", b=16)[:, :, 3]
                c3 = e_[:, 0:64].rearrange("p (b l) -> p b l", b=16)[:, :, 3]
                s3 = d_[:, 0:64].rearrange("p (b l) -> p b l", b=16)[:, :, 3]
                V(lambda e: e.tensor_tensor(out=C.QCLs[:, g, :], in0=q3, in1=c3, op=ALU.mult), [tq_, te_], [C.t_QLs])
                V(lambda e: e.tensor_tensor(out=C.QSLs[:, g, :], in0=q3, in1=s3, op=ALU.mult), [tq_, td_], [C.t_QLs])

        def xaxb(g):
            kc, k = g // 8, g % 8
            si = g % 2
            ba, bb_ = 2 * si, 2 * si + 1
            PE(lambda e: e.matmul(ps[ba][:, 0:N], lhsT=BpA[:, k, :], rhs=C.xn[:, kc, 0:N], start=True, stop=True), [t_Bp, C.t_xn], [t_ps[ba]])
            PE(lambda e: e.matmul(ps[bb_][:, 0:N], lhsT=BpB[:, k, :], rhs=C.xn[:, kc, 0:N], start=True, stop=True), [t_Bp, C.t_xn], [t_ps[bb_]])

        def ymm(g):
            k = g % 8
            si = g % 2
            qc_, qs_, tqc_ = C.qcb[si], C.qsb[si], C.t_qcs[si]
            yb = 5
            PE(lambda e: e.matmul(ps[yb][:, 0:N], lhsT=CpA[:, k, :], rhs=qc_[:, 0:N], start=(k == 0), stop=False), [t_Cp, tqc_], [t_ps[yb]])
            PE(lambda e: e.matmul(ps[yb][:, 0:N], lhsT=CpB[:, k, :], rhs=qs_[:, 0:N], start=False, stop=(k == 7)), [t_Cp, tqc_], [t_ps[yb]])

        if sample:
            yield from s5_sample_batched(C, jit)
        else:
            yield from s5_prompt_loop(C, jit, tables, xaxb, data, ymm, N)

        def consume(m, ba, tix):
            V(lambda e: e.tensor_tensor(out=tmpf[tix][:, 0:N], in0=ps[ba][:, 0:N], in1=tmpf[tix][:, 0:N], op=ALU.mult), [t_ps[ba], t_tmpf[tix]], [t_tmpf[tix]])
            V(lambda e: e.tensor_tensor(out=C.xres[:, m, 0:N], in0=C.xres[:, m, 0:N], in1=tmpf[tix][:, 0:N], op=ALU.add), [t_tmpf[tix], C.t_xres], [C.t_xres])

        yield from glu_dense(C.actb, C.t_actb, N, False, consume, "s5_w_glu")
        if last:
            PE(lambda e: e.matmul(ps[6][:, 0:64], lhsT=ja_f[:], rhs=QCL[:], start=True, stop=False), [t_QL, t_const], [t_ps[6]])
            PE(lambda e: e.matmul(ps[6][:, 0:64], lhsT=jb_f[:], rhs=QSL[:], start=False, stop=True), [t_QL, t_const], [t_ps[6]])
            A(lambda e: e.activation(out=tmpf[3][:, 0:64], in_=ps[6][:, 0:64], func=AF.Copy), [t_ps[6]], [t_tmpf[3]])
            PE(lambda e: e.transpose(out=ps[7][0:64, 0:128], in_=tmpf[3][:, 0:64], identity=ident_f[:]), [t_tmpf[3], t_const], [t_ps[7]])
            A(lambda e: e.activation(out=stage[1][0:64, 0:128], in_=ps[7][0:64, 0:128], func=AF.Copy), [t_ps[7]], [t_stage[1]])
            out_toks.append(S.dma("sp", s5rp, stage[1][0:64, 0:64], reads=[t_stage[1]]))
            out_toks.append(S.dma("sp", s5ip, stage[1][0:64, 64:128], reads=[t_stage[1]]))
        if sample:
            for hh in range(2):
                PE(lambda e, hh=hh: e.matmul(ps[hh][:], lhsT=ja_f[:], rhs=C.QCLs[:, 32 * hh:32 * hh + 32, :].rearrange("p g b -> p (g b)"), start=True, stop=False),
                   [C.t_QLs, t_const], [t_ps[hh]])
                PE(lambda e, hh=hh: e.matmul(ps[hh][:], lhsT=jb_f[:], rhs=C.QSLs[:, 32 * hh:32 * hh + 32, :].rearrange("p g b -> p (g b)"), start=False, stop=True),
                   [C.t_QLs, t_const], [t_ps[hh]])
                A(lambda e, hh=hh: e.activation(out=C.H0[:, 32 * hh:32 * hh + 32, :].rearrange("p g b -> p (g b)"), in_=ps[hh][:], func=AF.Copy), [t_ps[hh]], [C.t_H0])
            s5rs_v = s5rs.rearrange("b (g p) -> b g p", g=64)
            s5is_v = s5is.rearrange("b (g p) -> b g p", g=64)
            for r in range(16):
                for g4 in range(4):
                    g = 4 * r + g4
                    PE(lambda e, g=g, g4=g4: e.transpose(out=ps[6][0:16, g4 * 128:(g4 + 1) * 128], in_=C.H0[:, g, :], identity=ident_f[:]), [C.t_H0, t_const], [t_ps[6]])
                A(lambda e: e.activation(out=C.osm[:], in_=ps[6][0:16, :].rearrange("p (g c) -> p g c", g=4), func=AF.Copy), [t_ps[6]], [C.t_osm])
                out_toks.append(S.dma("sp", s5rs_v[:, 4 * r:4 * r + 4, :], C.osm[:, :, 0:64], reads=[C.t_osm]))
                out_toks.append(S.dma("sp", s5is_v[:, 4 * r:4 * r + 4, :], C.osm[:, :, 64:128], reads=[C.t_osm]))

    def psqb(bank, k):
        return ps[bank][:].bitcast(BF16)[:, (k % 4) * 128:(k % 4 + 1) * 128]

    def psq(bank, k):
        return ps[bank][:, (k % 4) * 128:(k % 4 + 1) * 128]

    def _diag(t):
        flat = t[:].rearrange("p k c -> p (k c)")
        return bass.AP(tensor=flat.tensor, offset=flat.offset, ap=[list(flat.ap[0]), [144, 8], [1, 16]])

    magic_t = sb("magic_t", [128, 1])
    V(lambda e: e.memset(magic_t[:], MAGIC), [], [t_const])

    def sgn_magic():
        return magic_t[:, 0:1]

    order = [4, 0, 1, 2, 3]
    gens = [run_tile(order[p], ctxs[p % 2]) for p in range(5)]
    acc = [0.0] * 5
    fin = [False] * 5
    marks = [set() for _ in range(5)]
    blocked = [None] * 5
    active = []
    started = 0
    while True:
        while started < 5 and len(active) < 2 and (started < 2 or fin[started - 2]) and (started <= 1 or "in_s5" in marks[started - 1] or fin[started - 1]):
            acc[started] = max([acc[t] for t in active], default=0.0)
            active.append(started)
            started += 1
        if not active:
            break
        for t in active:
            if blocked[t] is not None:
                key = blocked[t]
                if t == 0 or fin[t - 1] or key in marks[t - 1]:
                    blocked[t] = None
                    acc[t] = max(acc[t], max([acc[x] for x in active if x != t], default=0.0))
        cands = [t for t in active if blocked[t] is None]
        assert cands, (active, blocked)
        t = min(cands, key=lambda x: acc[x])
        try:
            r = next(gens[t])
        except StopIteration:
            fin[t] = True
            active.remove(t)
            continue
        if isinstance(r, str):
            kind, key = r.split(":")
            if kind == "done":
                marks[t].add(key)
            else:
                blocked[t] = key
        else:
            acc[t] += r
    S.wait_all("sp", out_toks)
    if record:
        st.close()
        return None, wplan
    S.run(st)
    st.close()
    return nc, wplan


_CACHE = {}


def kernel(**inp):
    f = lambda a: np.ascontiguousarray(np.asarray(a, dtype=np.float32))
    if "nc" not in _CACHE:
        _, plan = build_nc(None)
        _CACHE["nc"], _ = build_nc(plan)
    nc = _CACHE["nc"]
    ident = np.eye(128, dtype=np.float32)
    iota = np.broadcast_to(np.arange(1, 513, dtype=np.float32), (128, 512)).copy()
    iotas = np.broadcast_to(np.tile(np.arange(1, 5, dtype=np.float32), 16), (128, 64)).copy()
    sgn = np.zeros((128, 4), np.float32)
    sgn[:64, 0], sgn[64:, 0] = 1.0, -1.0
    sgn[:64, 1], sgn[64:, 1] = -1.0, 1.0
    sgn[:, 2] = EPS
    sgn[:, 3] = math.pi / 2
    ja = np.eye(128, dtype=np.float32)
    jb = np.zeros((128, 128), np.float32)
    for p in range(64):
        jb[64 + p, p] = -1.0
        jb[p, 64 + p] = 1.0
    shared = {n: f(inp[n]) for n, _ in W_IN}
    shared.update(c_ident=ident, c_iota=iota, c_iotas=iotas, c_sgn=sgn, c_ja=ja, c_jb=jb)
    in_maps = []
    for c in range(8):
        m = dict(shared)
        m["xp"] = f(inp["x_prompt"][c])
        m["xs"] = f(inp["x_sample"][16 * c:16 * c + 16]).reshape(64, D)
        m["mem"] = f(inp["mem_prompt"][c])
        m["cc"] = f(inp["cache_conv"][0, 16 * c:16 * c + 16])
        m["sre"] = f(inp["state_s5_re"][0, 16 * c:16 * c + 16]).reshape(16, 4096)
        m["sim"] = f(inp["state_s5_im"][0, 16 * c:16 * c + 16]).reshape(16, 4096)
        m["cmk"] = f(inp["cache_mem_k"][:, 16 * c:16 * c + 16]).reshape(2, 16, 256, D)
        m["cmv"] = f(inp["cache_mem_v"][:, 16 * c:16 * c + 16]).reshape(2, 16, 256, D)
        in_maps.append(m)
    res = run_bass_kernel_spmd(nc, in_maps, core_ids=list(range(8))).results
    if DEBUG:
        _CACHE["dbg"] = np.asarray(res[0]["dbg"])
    cat = lambda k: np.stack([np.asarray(r[k]) for r in res])
    y_prompt = cat("yp")
    y_sample = cat("ys").reshape(128, 4, D)
    conv_prompt = cat("convp")[None]
    conv_sample = cat("convs").reshape(1, 128, 30, D)
    s5_re_p, s5_im_p = cat("s5rp")[None], cat("s5ip")[None]
    s5_re_s = cat("s5rs").reshape(1, 128, 64, 64)
    s5_im_s = cat("s5is").reshape(1, 128, 64, 64)
    mk = cat("mkp").transpose(1, 0, 2, 3).reshape(2, 8, 256, 4, 256)
    mv = cat("mvp").transpose(1, 0, 2, 3).reshape(2, 8, 256, 4, 256)
    return (y_prompt, y_sample, conv_prompt, conv_sample, s5_re_p, s5_im_p, s5_re_s, s5_im_s, mk, mv)
```
